# Optimizing a Trainium2 kernel written in Bass

```python
import functools
import jax, jax.numpy as jnp
from jax import lax
import numpy as np

D_MODEL = 1024
BATCH = 8
SEQ = 8192
DEPTH = 1
DEC_BATCH = 8
DEC_SEQ = 32
PAST_LEN = 4096

CHUNK = 64
H_A = 4
DK_A = 128
DV_A = 256
GATE_RANK = 16
GATE_TAU = 16.0
H_B = 8
HD_B = 64
BAND_CHUNKS = 8
REL_CLIP = 128
D_FF = 2816
CONV_W = 3
PLE_DIM = 256
EPS = 1e-6

QK_A = H_A * DK_A
V_A = H_A * DV_A
W_B = H_B * HD_B
REACH = BAND_CHUNKS * CHUNK
IN_SIZES = (QK_A, QK_A, V_A, V_A, GATE_RANK, W_B, W_B, W_B, D_MODEL, D_MODEL)
N_IN = sum(IN_SIZES)

kernel_name = 'hybrid_gla_chunkband_stream_step'


def rms_norm(x, g):
    xf = x.astype(jnp.float32)
    y = xf * lax.rsqrt(jnp.mean(xf * xf, axis=-1, keepdims=True) + EPS) * g.astype(jnp.float32)
    return y.astype(x.dtype)


def split_cols(z):
    idx = np.cumsum(IN_SIZES)[:-1].tolist()
    return jnp.split(z, idx, axis=-1)


def gla_branch(q, k, v, r, alr, w_a2, b_a2, g_gla, s0, block):
    B, T, _ = q.shape
    n = T // block
    f32 = jnp.float32
    log_a = jax.nn.log_sigmoid((alr @ w_a2 + b_a2).astype(f32)) / GATE_TAU

    def to_blocks(t, e):
        return t.astype(f32).reshape(B, n, block, H_A, e).transpose(1, 0, 3, 2, 4)

    qb = to_blocks(q, DK_A) * DK_A ** -0.5
    kb = to_blocks(k, DK_A)
    vb = to_blocks(v, DV_A)
    ab = to_blocks(log_a, DK_A)
    mask = jnp.tril(jnp.ones((block, block), dtype=bool))[:, :, None]

    def step(S, inp):
        qi, ki, vi, ai = inp
        b = jnp.cumsum(ai, axis=2)
        decay = jnp.exp(jnp.where(mask, b[:, :, :, None, :] - b[:, :, None, :, :], -jnp.inf))
        scores = jnp.einsum('bhic,bhjc,bhijc->bhij', qi, ki, decay)
        o = (jnp.einsum('bhij,bhjv->bhiv', scores, vi)
             + jnp.einsum('bhic,bhcv->bhiv', qi * jnp.exp(b), S))
        bl = b[:, :, -1]
        S = (jnp.exp(bl)[..., None] * S
             + jnp.einsum('bhjc,bhjv->bhcv', ki * jnp.exp(bl[:, :, None] - b), vi))
        return S, o

    s_fin, o = lax.scan(step, s0.astype(f32), (qb, kb, vb, ab))
    o = o.transpose(1, 0, 3, 2, 4).reshape(B, T, H_A, DV_A)
    o = o * lax.rsqrt(jnp.mean(o * o, axis=-1, keepdims=True) + EPS)
    o = o.reshape(B, T, V_A) * g_gla.astype(f32)
    o = o.astype(q.dtype) * jax.nn.silu(r)
    return o, s_fin.astype(s0.dtype)


def band_attend(q, k, v, q_pos, k_pos, rel_bias):
    s = jnp.einsum('bqhe,bkhe->bhqk', q, k).astype(jnp.float32) * HD_B ** -0.5
    rel = jnp.clip(q_pos[:, None] - k_pos[None, :], -REL_CLIP, REL_CLIP) + REL_CLIP
    bias = rel_bias.astype(jnp.float32)[:, rel]
    qc = q_pos // CHUNK
    kc = k_pos // CHUNK
    vis = ((k_pos[None, :] >= 0) & (kc[None, :] <= qc[:, None])
           & (kc[None, :] >= qc[:, None] - BAND_CHUNKS))
    s = jnp.where(vis, s + bias, -jnp.inf)
    p = jax.nn.softmax(s, axis=-1)
    return jnp.einsum('bhqk,bkhe->bqhe', p.astype(v.dtype), v)


def attn_prompt(q, k, v, rel_bias):
    B, T, _ = q.shape
    qh = q.reshape(B, T, H_B, HD_B)
    kh = k.reshape(B, T, H_B, HD_B)
    vh = v.reshape(B, T, H_B, HD_B)
    pad = ((0, 0), (REACH, 0), (0, 0), (0, 0))
    kp = jnp.pad(kh, pad)
    vp = jnp.pad(vh, pad)

    def one_chunk(c):
        start = c * CHUNK
        qc = lax.dynamic_slice_in_dim(qh, start, CHUNK, axis=1)
        kc = lax.dynamic_slice_in_dim(kp, start, REACH + CHUNK, axis=1)
        vc = lax.dynamic_slice_in_dim(vp, start, REACH + CHUNK, axis=1)
        q_pos = start + jnp.arange(CHUNK, dtype=jnp.int32)
        k_pos = start - REACH + jnp.arange(REACH + CHUNK, dtype=jnp.int32)
        return band_attend(qc, kc, vc, q_pos, k_pos, rel_bias)

    o = lax.map(one_chunk, jnp.arange(T // CHUNK, dtype=jnp.int32))
    o = o.transpose(1, 0, 2, 3, 4).reshape(B, T, W_B)
    keep = min(REACH, T)
    return o, kh[:, T - keep:], vh[:, T - keep:]


def attn_sample(q, k, v, rel_bias, cache_k, cache_v):
    B, T, _ = q.shape
    qh = q.reshape(B, T, H_B, HD_B)
    kh = k.reshape(B, T, H_B, HD_B)
    vh = v.reshape(B, T, H_B, HD_B)
    lc = cache_k.shape[1]
    k_all = jnp.concatenate([cache_k.astype(kh.dtype), kh], axis=1)
    v_all = jnp.concatenate([cache_v.astype(vh.dtype), vh], axis=1)
    k_pos = jnp.concatenate([PAST_LEN - lc + jnp.arange(lc, dtype=jnp.int32),
                             PAST_LEN + jnp.arange(T, dtype=jnp.int32)])
    q_pos = PAST_LEN + jnp.arange(T, dtype=jnp.int32)
    o = band_attend(qh, k_all, v_all, q_pos, k_pos, rel_bias)
    return o.reshape(B, T, W_B), kh, vh


def conv_ffn(h, conv_s0, w_up, conv_w, conv_b, w_down):
    T = h.shape[1]
    a, g = jnp.split(h @ w_up, [D_FF], axis=-1)
    gx = jnp.concatenate([conv_s0.astype(g.dtype), g], axis=1)
    gc = conv_b + sum(conv_w[i] * gx[:, i:i + T] for i in range(CONV_W))
    y = (jax.nn.gelu(gc) * a) @ w_down
    return y, gx[:, T:]


def layer_forward(x, pe, attn_fn, gla_s0, gla_block, conv_s0, w):
    (g_pre_mix, w_in, w_a2, b_a2, g_gla, rel_bias, w_br_a, w_br_b, w_out, g_post_mix,
     g_pre_ffn, w_up, conv_w, conv_b, w_down, g_post_ffn,
     g_pre_ple, w_ple_gate, w_ple, g_post_ple) = w
    h = rms_norm(x, g_pre_mix)
    qa, ka, va, ra, alr, qb, kb, vb, ga, gb = split_cols(h @ w_in)
    oa, s_gla = gla_branch(qa, ka, va, ra, alr, w_a2, b_a2, g_gla, gla_s0, gla_block)
    ob, k_rows, v_rows = attn_fn(qb, kb, vb, rel_bias)
    mix = (jax.nn.sigmoid(ga) * (oa @ w_br_a) + jax.nn.sigmoid(gb) * (ob @ w_br_b)) @ w_out
    x = x + rms_norm(mix, g_post_mix)
    f, conv_new = conv_ffn(rms_norm(x, g_pre_ffn), conv_s0, w_up, conv_w, conv_b, w_down)
    x = x + rms_norm(f, g_post_ffn)
    gate = jax.nn.sigmoid(rms_norm(x, g_pre_ple) @ w_ple_gate)
    x = x + rms_norm(gate * (pe @ w_ple), g_post_ple)
    return x, k_rows, v_rows, s_gla, conv_new


def setup_inputs(seed: int = 0) -> dict:
    key = jax.random.key(seed)
    ks = iter(jax.random.split(key, 32))
    nrm = lambda shape, scale: scale * jax.random.normal(next(ks), shape, jnp.float32)
    gain = lambda shape: 1.0 + nrm(shape, 0.01)
    lc = min(REACH, PAST_LEN)
    return {
        'x_prompt': nrm((BATCH, SEQ, D_MODEL), 1.0),
        'x_sample': nrm((DEC_BATCH, DEC_SEQ, D_MODEL), 1.0),
        'cache_attn_k': nrm((DEPTH, DEC_BATCH, lc, H_B, HD_B), 1.0),
        'cache_attn_v': nrm((DEPTH, DEC_BATCH, lc, H_B, HD_B), 1.0),
        'state_gla': nrm((DEPTH, DEC_BATCH, H_A, DK_A, DV_A), 1.0),
        'state_conv': nrm((DEPTH, DEC_BATCH, CONV_W - 1, D_FF), 1.0),
        'p_prompt': nrm((DEPTH, BATCH, SEQ, PLE_DIM), 1.0),
        'p_sample': nrm((DEPTH, DEC_BATCH, DEC_SEQ, PLE_DIM), 1.0),
        'g_pre_mix': gain((DEPTH, D_MODEL)),
        'w_in': nrm((DEPTH, D_MODEL, N_IN), D_MODEL ** -0.5),
        'w_a2': nrm((DEPTH, GATE_RANK, QK_A), GATE_RANK ** -0.5),
        'b_a2': nrm((DEPTH, QK_A), 0.1),
        'g_gla': gain((DEPTH, V_A)),
        'rel_bias': nrm((DEPTH, H_B, 2 * REL_CLIP + 1), 0.1),
        'w_br_a': nrm((DEPTH, V_A, D_MODEL), V_A ** -0.5),
        'w_br_b': nrm((DEPTH, W_B, D_MODEL), W_B ** -0.5),
        'w_out': nrm((DEPTH, D_MODEL, D_MODEL), D_MODEL ** -0.5),
        'g_post_mix': gain((DEPTH, D_MODEL)),
        'g_pre_ffn': gain((DEPTH, D_MODEL)),
        'w_up': nrm((DEPTH, D_MODEL, 2 * D_FF), D_MODEL ** -0.5),
        'conv_w': nrm((DEPTH, CONV_W, D_FF), CONV_W ** -0.5),
        'conv_b': nrm((DEPTH, D_FF), 0.02),
        'w_down': nrm((DEPTH, D_FF, D_MODEL), D_FF ** -0.5),
        'g_post_ffn': gain((DEPTH, D_MODEL)),
        'g_pre_ple': gain((DEPTH, D_MODEL)),
        'w_ple_gate': nrm((DEPTH, D_MODEL, D_MODEL), D_MODEL ** -0.5),
        'w_ple': nrm((DEPTH, PLE_DIM, D_MODEL), PLE_DIM ** -0.5),
        'g_post_ple': gain((DEPTH, D_MODEL)),
    }


def reference(x_prompt, x_sample, cache_attn_k, cache_attn_v, state_gla, state_conv,
              p_prompt, p_sample, g_pre_mix, w_in, w_a2, b_a2, g_gla, rel_bias,
              w_br_a, w_br_b, w_out, g_post_mix, g_pre_ffn, w_up, conv_w, conv_b,
              w_down, g_post_ffn, g_pre_ple, w_ple_gate, w_ple, g_post_ple):
    bp = x_prompt.shape[0]
    ts = x_sample.shape[1]
    yp, ys = x_prompt, x_sample
    kp_l, vp_l, sp_l, cp_l, ks_l, vs_l, ss_l, cs_l = [], [], [], [], [], [], [], []
    for l in range(DEPTH):
        w = (g_pre_mix[l], w_in[l], w_a2[l], b_a2[l], g_gla[l], rel_bias[l], w_br_a[l],
             w_br_b[l], w_out[l], g_post_mix[l], g_pre_ffn[l], w_up[l], conv_w[l],
             conv_b[l], w_down[l], g_post_ffn[l], g_pre_ple[l], w_ple_gate[l], w_ple[l],
             g_post_ple[l])
        yp, kp, vp, sp, cp = layer_forward(
            yp, p_prompt[l], attn_prompt,
            jnp.zeros((bp, H_A, DK_A, DV_A), jnp.float32), CHUNK,
            jnp.zeros((bp, CONV_W - 1, D_FF), yp.dtype), w)
        ys, ks, vs, ss, cs = layer_forward(
            ys, p_sample[l],
            functools.partial(attn_sample, cache_k=cache_attn_k[l], cache_v=cache_attn_v[l]),
            state_gla[l], ts, state_conv[l], w)
        kp_l.append(kp); vp_l.append(vp); sp_l.append(sp); cp_l.append(cp)
        ks_l.append(ks); vs_l.append(vs); ss_l.append(ss); cs_l.append(cs)
    return (yp, ys,
            jnp.stack(kp_l), jnp.stack(vp_l), jnp.stack(sp_l), jnp.stack(cp_l),
            jnp.stack(ks_l), jnp.stack(vs_l), jnp.stack(ss_l), jnp.stack(cs_l))
```

```python
import contextlib
import numpy as np
import concourse.bass as bass
import concourse.mybir as mybir
from concourse.bass_utils import run_bass_kernel_spmd

F32 = mybir.dt.float32
BF16 = mybir.dt.bfloat16
AF = mybir.ActivationFunctionType
ALU = mybir.AluOpType

D = 1024
SEQ = 8192
TS = 32
T = 512
H_A, DK, DV = 4, 128, 256
H_B, HD = 8, 64
DFF = 2816
NFF = 22
PLE = 256
EPS = 1e-6
NIN = 6672
C_QA, C_KA, C_VA, C_RA, C_ALR, C_QB, C_KB, C_VB, C_GA, C_GB = 0, 512, 1024, 2048, 3072, 3088, 3600, 4112, 4624, 5648
NSLOT = 3
SLOT_F = 4096
NEG = -30000.0


class Op:
    __slots__ = ("eng", "fn", "deps", "inc", "dsem", "dval", "group", "ev", "cost", "lat", "odeps", "line", "st")

    def __init__(self, eng, fn, dsem=None, group=False):
        self.eng, self.fn, self.dsem, self.group = eng, fn, dsem, group
        self.deps = []
        self.odeps = []
        self.cost = 0.0
        self.lat = 0.0
        self.inc = False
        self.dval = 0
        self.ev = None


class Prog:
    ENGS = ("pe", "act", "dve", "pool", "sp")

    def __init__(self):
        self.ops = {e: [] for e in self.ENGS}
        self.lastw = {}
        self.readers = {}
        self.dcount = {}
        self.nadd = 0
        self.dbg = False
        self.limit = None
        self.labels = []

    DEF_COST = dict(pe=2.0, act=0.65, dve=0.7, pool=1.0, sp=0.05)

    def add(self, eng, fn, reads=(), writes=(), dsem=None, group=False, cost=None, lat=None):
        op = Op(eng, fn, dsem, group)
        import sys as _s3
        fr = _s3._getframe(1)
        op.line = (fr.f_lineno, fr.f_back.f_lineno if fr.f_back else 0)
        if dsem is not None:
            op.cost = 0.05 if eng != "pool" else 0.3
            op.lat = 4.0 if lat is None else lat
        else:
            op.cost = self.DEF_COST[eng] if cost is None else cost
            op.lat = 0.15
        if fn is None:
            op.cost = 0.0
        self.nadd += 1
        import os as _os2, sys as _sys2
        if _os2.environ.get("KSHOW") and abs(self.nadd - int(_os2.environ["KSHOW"])) <= 2:
            fr = _sys2._getframe(1)
            print("OP", self.nadd, eng, "line", fr.f_lineno, "caller", fr.f_back.f_lineno if fr.f_back else None)
        if self.limit is not None and self.nadd > self.limit and fn is not None:
            return op
        psr = [b for b in reads if isinstance(b, tuple) and b[0] == "ps"]
        if psr:
            reads = [b for b in reads if not (isinstance(b, tuple) and b[0] == "ps")]
            writes = list(writes) + [b for b in psr if b not in writes]
        seen = set()
        for b in reads:
            w = self.lastw.get(b)
            if w is not None and id(w) not in seen:
                seen.add(id(w))
                op.deps.append(w)
        for b in writes:
            w = self.lastw.get(b)
            if w is not None and id(w) not in seen:
                if not (group and w.dsem is dsem):
                    seen.add(id(w))
                    op.deps.append(w)
            for r in self.readers.get(b, ()):
                if id(r) not in seen:
                    seen.add(id(r))
                    op.deps.append(r)
        op.odeps = list(op.deps)
        if eng == "pe":
            op.deps = [d for d in op.deps if not (d.dsem is None and d.eng == "pe")]
        for b in writes:
            self.lastw[b] = op
            self.readers[b] = []
        for b in reads:
            if isinstance(b, tuple) and b[0] == "const":
                continue
            self.readers.setdefault(b, []).append(op)
        if dsem is not None:
            k = id(dsem)
            self.dcount[k] = self.dcount.get(k, 0) + 16
            op.dval = self.dcount[k]
        self.ops[eng].append(op)
        if dsem is not None:
            self.dsems = getattr(self, "dsems", {})
            self.dsems[id(dsem)] = dsem
        return op

    def schedule(self, window):
        pending = {e: list(self.ops[e]) for e in self.ENGS}
        sched = {e: [] for e in self.ENGS}
        tnow = {e: 0.0 for e in self.ENGS}
        done = {}
        remaining = sum(len(v) for v in pending.values())
        while remaining:
            best = None
            for e in self.ENGS:
                pe_ = pending[e]
                te = tnow[e]
                for i in range(min(window[e], len(pe_))):
                    c = pe_[i]
                    rt = 0.0
                    ok = True
                    for d in c.odeps:
                        t = done.get(id(d))
                        if t is None:
                            ok = False
                            break
                        if t > rt:
                            rt = t
                    if not ok:
                        continue
                    st = rt if rt > te else te
                    key = (st, i)
                    if best is None or key < best[0]:
                        best = (key, e, i, c, st)
                    if rt <= te:
                        break
            if best is None:
                raise RuntimeError("scheduler deadlock")
            _, e, i, c, st = best
            c.st = st
            if self.dbg and e == "pe" and st > tnow[e] + 1.5:
                blk = max(c.odeps, key=lambda d: done[id(d)])
                print(f"PE gap {st - tnow[e]:5.1f}us at t={st:8.1f} op line {c.line} waits for {blk.eng} line {blk.line} (started {blk.st:.1f}, cost {blk.cost})")
            pending[e].pop(i)
            sched[e].append(c)
            tnow[e] = st + c.cost
            done[id(c)] = st + c.cost + c.lat
            remaining -= 1
        self.ops = sched
        self.est_time = max(tnow.values())

    def emit(self, nc, block, sems):
        for e in self.ENGS:
            for op in self.ops[e]:
                for d in op.deps:
                    if d.dsem is None:
                        d.inc = True
        for e in self.ENGS:
            c = 0
            for op in self.ops[e]:
                if op.dsem is not None:
                    v = self.dcount[id(op.dsem)] if op.group else op.dval
                    op.ev = (op.dsem, v)
                else:
                    if op.inc:
                        c += 1
                    op.ev = (sems[e], c) if op.inc else None

        def run(e, eng):
            waited = {}
            for op in self.ops[e]:
                for d in op.deps:
                    s, v = d.ev
                    if waited.get(id(s), 0) < v:
                        eng.wait_ge(s, v)
                        waited[id(s)] = v
                if op.fn is None:
                    continue
                ins = op.fn(eng)
                if op.dsem is not None:
                    ins.then_inc(op.dsem, 16)
                elif op.inc:
                    ins.then_inc(sems[e], 1)
            if e == "sp":
                for k, dsem in getattr(self, "dsems", {}).items():
                    eng.wait_ge(dsem, self.dcount[k])

        block.tensor(lambda eng: run("pe", eng))
        block.scalar(lambda eng: run("act", eng))
        block.vector(lambda eng: run("dve", eng))
        block.gpsimd(lambda eng: run("pool", eng))
        block.sync(lambda eng: run("sp", eng))


def bc_last(ap, n):
    l = [list(x) for x in ap.ap]
    assert l[-1][1] == 1
    l[-1] = [0, n]
    return bass.AP(ap.tensor, ap.offset, l)


def bc_new(ap, n):
    l = [list(x) for x in ap.ap] + [[0, n]]
    return bass.AP(ap.tensor, ap.offset, l)


def bc_mid(ap, n):
    l = [list(x) for x in ap.ap]
    l = l[:1] + [[0, n]] + l[1:]
    return bass.AP(ap.tensor, ap.offset, l)


def weight_blocks():
    B = []

    def full(name, w, c0, group):
        B.append(dict(name=name, group=group, F=8 * 512, pieces=[(w, c0, 512, 8, 0, 512)]))

    full("qa", "w_in", C_QA, 0)
    full("ka", "w_in", C_KA, 0)
    full("va0", "w_in", C_VA, 0)
    full("va1", "w_in", C_VA + 512, 0)
    full("ra0", "w_in", C_RA, 0)
    full("ra1", "w_in", C_RA + 512, 0)
    full("qb", "w_in", C_QB, 1)
    full("kb", "w_in", C_KB, 1)
    full("vb", "w_in", C_VB, 1)
    for g in range(4):
        B.append(dict(name=f"mixa{g}", group=2, F=4096,
                      pieces=[("w_in", C_GA + g * 256, 256, 8, 0, 512), ("w_br_a", g * 256, 256, 8, 256, 512)]))
        B.append(dict(name=f"mixb{g}", group=2, F=3072,
                      pieces=[("w_in", C_GB + g * 256, 256, 8, 0, 256), ("w_br_b", g * 256, 256, 4, 2048, 256)]))
    full("wo0", "w_out", 0, 3)
    full("wo1", "w_out", 512, 3)
    for jb in range(NFF // 2):
        pcs = []
        for jj in range(2):
            j = jb * 2 + jj
            pcs.append(("w_up", j * 128, 128, 8, jj * 256, 512))
            pcs.append(("w_up", DFF + j * 128, 128, 8, jj * 256 + 128, 512))
        B.append(dict(name=f"up{jb}", group=4, F=4096, pieces=pcs))
    for half in range(2):
        for kb, (k0, kn) in enumerate(((0, 8), (8, 8), (16, 6))):
            B.append(dict(name=f"dn{half}_{kb}", group=5, F=kn * 512, k0=k0, kn=kn,
                          pieces=[("w_down", half * 512, 512, kn, 0, 512, k0)]))
    full("pg0", "w_ple_gate", 0, 6)
    full("pg1", "w_ple_gate", 512, 6)
    B.append(dict(name="pl", group=6, F=2048, pieces=[("w_ple", 0, 1024, 2, 0, 1024)]))
    off = 0
    for b in B:
        b["off"] = off
        off += 128 * SLOT_F
    return B, off


WSHAPES = dict(w_in=(D, NIN), w_br_a=(D, D), w_br_b=(512, D), w_out=(D, D), w_up=(D, 2 * DFF),
               w_down=(DFF, D), w_ple_gate=(D, D), w_ple=(PLE, D))


def build(nt, debug=False):
    nc = bass.Bass("TRN2", target_bir_lowering=False)
    seq = nt * T
    P = Prog()
    import os as _os
    if _os.environ.get("KLIMIT"):
        P.limit = int(_os.environ["KLIMIT"])
    WB, scr_elems = weight_blocks()
    NBLK = len(WB)

    def din(name, shape, dt=F32):
        return nc.dram_tensor(name, list(shape), dt, kind="ExternalInput")

    def dout(name, shape, dt=F32):
        return nc.dram_tensor(name, list(shape), dt, kind="ExternalOutput")

    xp = din("xp", [seq, D]).ap()
    pp = din("pp", [seq, PLE]).ap()
    xs = din("xs", [TS, D]).ap()
    pps = din("pps", [TS, PLE]).ap()
    ck = din("ck", [512, 512]).ap()
    cv = din("cv", [512, 512]).ap()
    sg = din("sg", [H_A, DK, DV]).ap()
    sc = din("sc", [2, DFF]).ap()
    wd = {k: din(k, v) for k, v in WSHAPES.items()}
    w_a2 = din("w_a2", [16, 512]).ap()
    gvec = din("gvec", [7, D])
    b_a2 = din("b_a2", [512]).ap()
    conv_w = din("conv_w", [3, DFF]).ap()
    conv_b = din("conv_b", [DFF]).ap()
    biasT_d = din("biasT", [5, 2, 128, 512]).ap()
    cmat = din("cmat", [128, 128 + 128 + 512]).ap()

    yp = dout("yp", [seq, D]).ap()
    ys = dout("ys", [TS, D]).ap()
    kp = dout("kp", [512, 512]).ap()
    vp = dout("vp", [512, 512]).ap()
    spo = dout("spo", [H_A, DK, DV]).ap()
    cpo = dout("cpo", [2, DFF]).ap()
    kso = dout("kso", [TS, 512]).ap()
    vso = dout("vso", [TS, 512]).ap()
    sso = dout("sso", [H_A, DK, DV]).ap()
    cso = dout("cso", [2, DFF]).ap()
    scr = nc.dram_tensor("wscr", [scr_elems], BF16, kind="Internal")

    es = contextlib.ExitStack()
    with es:
        def sb(name, shape, dt):
            return es.enter_context(nc.sbuf_tensor("sb_" + name, list(shape), dt))

        xbuf = [sb(f"xbuf{i}", [128, 4, D], F32) for i in range(2)]
        pbuf = sb("pbuf", [128, 4, PLE], F32)
        S_f = sb("S_f", [128, H_A, DV], F32)
        S_b = sb("S_b", [128, H_A, DV], BF16)
        kbT = sb("kbT", [128, 4, 1024], BF16)
        vbr = sb("vbr", [128, 8, 520], BF16)
        biasT = sb("biasT", [128, 5, 2, 512], BF16)
        gpost = sb("gpost", [128, 3, D], F32)
        ggla = sb("ggla", [128, D], F32)
        ident = sb("ident", [128, 128], BF16)
        tri = sb("tri", [128, 128], F32)
        smask = sb("smask", [128, 512], F32)
        gpre = sb("gpre", [128, 3, 8], F32)
        negb = sb("negb", [128, 4], F32)
        cw = sb("cw", [128, 3, NFF], F32)
        cb = sb("cb", [128, NFF], F32)
        walr = sb("walr", [128, 8, 16], BF16)
        wa2 = sb("wa2", [16, 512], BF16)
        ones64 = sb("ones64", [128, 64], BF16)
        carry = sb("carry", [128, NFF, 2], F32)
        wring = [sb(f"wr{i}", [128, SLOT_F], BF16) for i in range(NSLOT)]
        hnT = sb("hnT", [128, 8, T], BF16)
        hntok2 = [sb(f"hntok{i}", [128, D], BF16) for i in range(2)]
        junk2 = [sb(f"junk{i}", [128, D], BF16) for i in range(2)]
        oaT = sb("oaT", [128, 8, T], BF16)
        obT = sb("obT", [128, 4, T], BF16)
        stat2 = sb("stat", [128, 128], F32)
        R1N = 32 * 1024
        R1 = sb("R1", [128, R1N], BF16)
        eps_t = sb("eps_t", [128, 1], F32)
        mhalf = sb("mhalf", [128, 8], F32)
        eblt = sb("eblt", [128, H_A, 4], F32)

        def r1(off_kb, nelem, dt, shape=None):
            esz = 2 if dt == BF16 else 4
            o = off_kb * 512
            n = nelem * esz // 2
            ap = R1[:, o:o + n]
            if dt == F32:
                ap = ap.bitcast(F32)
            ids = [("R1", i) for i in range(off_kb, off_kb + (nelem * esz + 1023) // 1024)]
            return ap, ids

        pd = [es.enter_context(nc.psum_tensor(f"pd{i}", [128, 1024], F32)) for i in range(4)]

        def bank(i):
            return pd[i // 2][:, (i % 2) * 512:(i % 2) * 512 + 512]

        def bankb(i):
            return bank(i).bitcast(BF16)

        def pid(i):
            return ("ps", i)

        sems = {e: es.enter_context(nc.semaphore(f"sem_{e}")) for e in Prog.ENGS}
        s_setup = es.enter_context(nc.semaphore("s_setup"))
        s_setup2 = es.enter_context(nc.semaphore("s_setup2"))
        s_conv = [es.enter_context(nc.semaphore(f"s_conv{i}")) for i in range(NBLK)]
        s_slot = [es.enter_context(nc.semaphore(f"s_slot{i}")) for i in range(NSLOT)]
        s_x = [es.enter_context(nc.semaphore(f"s_x{i}")) for i in range(2)]
        s_p = es.enter_context(nc.semaphore("s_p"))
        s_y = [es.enter_context(nc.semaphore(f"s_y{i}")) for i in range(2)]
        s_ko = es.enter_context(nc.semaphore("s_ko"))
        s_vo = es.enter_context(nc.semaphore("s_vo"))
        s_o = [es.enter_context(nc.semaphore(f"s_o{i}")) for i in range(5)]
        s_l = [es.enter_context(nc.semaphore(f"s_l{i}")) for i in range(3)]
        s_c = [es.enter_context(nc.semaphore(f"s_c{i}")) for i in range(6)]

        CONST = ("const", 0)

        def setup_dma(out, in_, eng="sp", nonc=False):
            def fn(e):
                if nonc:
                    with nc.allow_non_contiguous_dma(reason="small const"):
                        return e.dma_start(out=out, in_=in_)
                return e.dma_start(out=out, in_=in_)
            return P.add(eng, fn, writes=[CONST], dsem=s_setup, group=True)

        cst, cst_ids = r1(40, 768, F32)
        P.add("sp", lambda e: e.dma_start(out=cst, in_=cmat), writes=cst_ids + [CONST], dsem=s_setup, group=True)
        for i, row in enumerate((1, 3, 5)):
            setup_dma(gpost[:, i, :], bass.AP(gvec, row * D, [[0, 128], [1, D]]))
        setup_dma(ggla[:], bass.AP(gvec, 6 * D, [[0, 128], [1, D]]))
        for i, row in enumerate((0, 2, 4)):
            setup_dma(gpre[:, i, :], bass.AP(gvec, row * D, [[1, 128], [128, 8]]), nonc=True)
        setup_dma(negb[:], b_a2.rearrange("(h p) -> p h", p=128), nonc=True)
        setup_dma(cw[:], conv_w.rearrange("i (j p) -> p i j", p=128), nonc=True)
        setup_dma(cb[:], conv_b.rearrange("(j p) -> p j", p=128), nonc=True)
        bst, bst_ids = r1(0, 5 * 2 * 512, F32)
        bst3 = bst.rearrange("p (a b) -> p a b", b=512)
        P.add("sp", lambda e: e.dma_start(out=bst3, in_=biasT_d.rearrange("k r p q -> p (k r) q")),
              writes=bst_ids + [CONST], dsem=s_setup, group=True)
        P.add("pool", lambda e: e.dma_start(out=walr[:], in_=wd["w_in"].ap().rearrange("(kc p) n -> p kc n", p=128)[:, :, C_ALR:C_ALR + 16]),
              writes=[("const", 1)], dsem=s_setup2, group=True)
        P.add("pool", lambda e: e.dma_start(out=wa2[:], in_=w_a2), writes=[("const", 1)], dsem=s_setup2, group=True)

        name2idx = {b["name"]: i for i, b in enumerate(WB)}
        HB = ["va0", "va1"]
        B2B = ["pg0", "pg1", "pl"]
        B1B = [b["name"] for b in WB if b["name"] not in HB + B2B]
        conv_pending = [name2idx[n] for n in HB + B1B + B2B]
        for bi in conv_pending:
            WB[bi]["group"] = bi

        def emit_conv(n=1):
            for _ in range(n):
                if not conv_pending:
                    return
                bi = conv_pending.pop(0)
                b = WB[bi]
                for pc_ in b["pieces"]:
                    wname, c0, ncols, KC, off, kst = pc_[:6]
                    k0 = pc_[6] if len(pc_) > 6 else 0
                    wv = wd[wname].ap().rearrange("(kc p) n -> p kc n", p=128)[:, k0:k0 + KC, c0:c0 + ncols]
                    dst = bass.AP(scr, b["off"] + off, [[SLOT_F, 128], [kst, KC], [1, ncols]])
                    P.add("pool", (lambda e, dst=dst, wv=wv: e.dma_start(out=dst, in_=wv)),
                          writes=[("scr", bi)], dsem=s_conv[bi], group=True, lat=6.0)
        emit_conv(int(_os.environ.get("KCONV0", 4)))

        P.add("dve", lambda e: e.tensor_copy(out=ident[:], in_=cst[:, 0:128]), reads=[CONST] + cst_ids, writes=[("c2", 0)])
        P.add("dve", lambda e: e.tensor_copy(out=tri[:], in_=cst[:, 128:256]), reads=[CONST] + cst_ids, writes=[("c2", 0)])
        P.add("dve", lambda e: e.tensor_copy(out=smask[:], in_=cst[:, 256:768]), reads=[CONST] + cst_ids, writes=[("c2", 0)])
        P.add("dve", lambda e: e.tensor_scalar(out=negb[:], in0=negb[:], scalar1=-1.0, scalar2=None, op0=ALU.mult),
              reads=[CONST], writes=[("c2", 0)])
        P.add("pool", lambda e: e.memset(ones64[:], 1.0), writes=[("c2", 1)])
        P.add("pool", lambda e: e.memset(vbr[:], 1.0), writes=[("vb", i) for i in range(8)])
        P.add("pool", lambda e: e.memset(carry[:], 0.0), writes=[("carryall",)] + [("carry", j) for j in range(NFF)])
        P.add("pool", lambda e: e.memset(S_f[:], 0.0), writes=["S_f"])
        P.add("pool", lambda e: e.memset(S_b[:], 0.0), writes=["S_b"])
        P.add("pool", lambda e: e.memset(eps_t[:], EPS), writes=[("c2", 1)])
        P.add("pool", lambda e: e.memset(mhalf[:], -0.5), writes=[("c2", 1)])
        P.add("dve", lambda e: e.tensor_scalar(out=gpost[:, 0, :], in0=gpost[:, 0, :], scalar1=0.5, scalar2=None, op0=ALU.mult), reads=[CONST], writes=[("c2", 0)])
        P.add("dve", lambda e: e.tensor_scalar(out=gpost[:, 2, :], in0=gpost[:, 2, :], scalar1=0.5, scalar2=None, op0=ALU.mult), reads=[CONST], writes=[("c2", 0)])
        for kb_ in range(5):
            P.add("pool", (lambda e, kb_=kb_: e.tensor_tensor(out=biasT[:, kb_, :, :], in0=bst3[:, 2 * kb_:2 * kb_ + 2, :], in1=bst3[:, 2:4, :], op=ALU.subtract)),
                  reads=bst_ids + [CONST], writes=[("c2", 2)])
        C2 = [CONST, ("const", 1), ("c2", 0), ("c2", 1), ("c2", 2)]

        ring = dict(next_load=0, next_use=0, total=0)
        seqblocks = []

        def ring_issue():
            i = ring["next_load"]
            if i >= len(seqblocks):
                return
            ring["next_load"] += 1
            b = WB[seqblocks[i]]
            slot = i % NSLOT
            F = b["F"]
            src = bass.AP(scr, b["off"], [[SLOT_F, 128], [1, F]])
            P.add("sp", (lambda e, slot=slot, F=F, src=src: e.dma_start(out=wring[slot][:, 0:F], in_=src)),
                  reads=[("scr", b["group"])], writes=[("w", slot)], dsem=s_slot[slot], lat=2.0 + F / 512.0)

        def ring_acquire(name):
            i = ring["next_use"]
            b = WB[seqblocks[i]]
            assert b["name"] == name, (b["name"], name)
            ring["next_use"] += 1
            slot = i % NSLOT
            return wring[slot], ("w", slot)

        def ring_release():
            emit_conv(1)
            ring_issue()

        def small_rstd(ss_ap, n, inv_n, reads, writes, out_ap):
            np_ = ss_ap.shape[0]
            P.add("pool", lambda e: e.tensor_scalar(out=out_ap, in0=ss_ap, scalar1=inv_n, scalar2=EPS, op0=ALU.mult, op1=ALU.add),
                  reads=reads + C2, writes=writes, cost=0.25)
            P.add("pool", lambda e: e.tensor_tensor(out=out_ap, in0=out_ap, in1=mhalf[0:np_, 0:n], op=ALU.pow),
                  reads=writes, writes=writes, cost=0.35)

        rr = dict(i=0)

        def next_bank(choices):
            rr["i"] += 1
            return choices[rr["i"] % len(choices)]

        def emit_tile(tc):
            nb, bt = tc["nb"], tc["bt"]
            TT = nb * bt
            xb = tc["xb"]
            xi = tc["xi"]
            xid = lambda b: ("x", xi, b)
            hid = lambda b: ("hnT", b)
            HN_ALL = [hid(b) for b in range(nb)]
            trb = [6, 7]
            par = tc["par"]
            stat = stat2[:, par * 64:(par + 1) * 64]
            SID = lambda k: ("stat", par, k)
            alt = dict(j=0, h=0)

            def jk():
                alt["j"] += 1
                i = alt["j"] % 2
                return junk2[i], ("junk", i)

            def hk():
                alt["h"] += 1
                i = alt["h"] % 2
                return hntok2[i], ("hntok", i)

            def prenorm(gi, dst_fn=None, dids=None):
                if dst_fn is None:
                    dst_fn = lambda b: hnT[:, :, b * bt:(b + 1) * bt]
                    dids = lambda b: [hid(b)]
                for b in range(nb):
                    jb_, jid = jk()
                    P.add("act", (lambda e, b=b, jb_=jb_: e.activation(out=jb_[0:bt, :], in_=xb[0:bt, b, :], func=AF.Square,
                                                                       accum_out=stat[0:bt, b:b + 1])),
                          reads=[xid(b)], writes=[SID((0, b)), jid], cost=1.25)
                    small_rstd(stat[0:bt, b:b + 1], 1, 1.0 / D, [SID((0, b))], [SID((1, b))], stat[0:bt, 4 + b:5 + b])
                for b in range(nb):
                    hb_, hbid = hk()
                    P.add("act", (lambda e, b=b, hb_=hb_: e.activation(out=hb_[0:bt, :], in_=xb[0:bt, b, :], func=AF.Copy,
                                                                       scale=stat[0:bt, 4 + b:5 + b])),
                          reads=[xid(b), SID((1, b))], writes=[hbid], cost=1.25)
                    bk = next_bank(trb)
                    pv = bankb(bk).rearrange("p (k t) -> p k t", t=128)

                    def trs(e, b=b, pv=pv, hb_=hb_):
                        ins = None
                        for kc in range(8):
                            ins = e.transpose(out=pv[:, kc, 0:bt], in_=hb_[0:bt, kc * 128:(kc + 1) * 128], identity=ident[0:bt, 0:bt])
                        return ins
                    P.add("pe", trs, reads=[hbid] + C2, writes=[pid(bk)], cost=0.9)
                    gb = bc_new(gpre[:, gi, :], bt)
                    P.add("dve", (lambda e, b=b, pv=pv, gb=gb: e.tensor_tensor(out=dst_fn(b), in0=pv[:, :, 0:bt], in1=gb, op=ALU.mult)),
                          reads=[pid(bk)] + C2, writes=dids(b), cost=1.2)

            def proj_F(slot_ap, wid, off, kst, KC, rhs_fn, rhs_ids, col0, banks):
                bk = next_bank(banks)
                ps = bank(bk)

                def fn(e):
                    ins = None
                    for kc in range(KC):
                        ins = e.matmul(ps[:, 0:TT], lhsT=slot_ap[:, off + kc * kst + col0: off + kc * kst + col0 + 128],
                                       rhs=rhs_fn(kc), start=(kc == 0), stop=(kc == KC - 1))
                    return ins
                P.add("pe", fn, reads=[wid] + rhs_ids, writes=[pid(bk)], cost=0.25 * KC)
                return bk, ps

            def proj_T(slot_ap, wid, off, kst, KC, lhs_fn, lhs_ids, b, ncols, banks, col0=0):
                bk = next_bank(banks)
                ps = bank(bk)

                def fn(e):
                    ins = None
                    for kc in range(KC):
                        ins = e.matmul(ps[0:bt, 0:ncols], lhsT=lhs_fn(kc, b),
                                       rhs=slot_ap[:, off + kc * kst + col0: off + kc * kst + col0 + ncols],
                                       start=(kc == 0), stop=(kc == KC - 1))
                    return ins
                P.add("pe", fn, reads=[wid] + lhs_ids, writes=[pid(bk)], cost=0.25 * KC)
                return bk, ps

            hn_rhs = lambda kc: hnT[:, kc, 0:TT]
            hn_lhs = lambda kc, b: hnT[:, kc, b * bt:(b + 1) * bt]
            PB = [0, 1, 2, 3]

            prenorm(0)
            va, va_ids = r1(12, 4 * 1024, BF16)
            va = va.rearrange("p (b c) -> p b c", c=1024)
            alrT, al_ids = r1(28, 512, BF16)
            bk = next_bank(PB)
            ps = bank(bk)

            def alr_fn(e, ps=ps):
                ins = None
                for kc in range(8):
                    ins = e.matmul(ps[0:16, 0:TT], lhsT=walr[:, kc, :], rhs=hnT[:, kc, 0:TT], start=(kc == 0), stop=(kc == 7))
                return ins
            P.add("pe", alr_fn, reads=HN_ALL + C2, writes=[pid(bk)], cost=3.0)
            P.add("act", (lambda e, ps=ps: e.activation(out=alrT[0:16, 0:TT], in_=ps[0:16, 0:TT], func=AF.Copy)),
                  reads=[pid(bk)], writes=al_ids)
            for half in range(2):
                slot, wid = ring_acquire(f"va{half}")
                for b in range(nb):
                    bk, ps = proj_T(slot, wid, 0, 512, 8, hn_lhs, [hid(b)], b, 512, PB)
                    P.add("act", (lambda e, ps=ps, b=b, half=half: e.activation(out=va[0:bt, b, half * 512:(half + 1) * 512], in_=ps[0:bt, :], func=AF.Copy)),
                          reads=[pid(bk)], writes=va_ids)
                ring_release()
            yield "head"
            psrc = tc["p_src"]
            P.add("sp", (lambda e: e.dma_start(out=pbuf[0:bt, 0:nb, :], in_=psrc)), writes=["pbuf"], dsem=s_p, lat=5.0)
            qaT, qa_ids = r1(0, 4 * 512, BF16)
            kaT, ka_ids = r1(4, 4 * 512, BF16)
            katok, kt_ids = r1(8, 4 * 512, BF16)
            silur, sr_ids = r1(20, 8 * 512, BF16)
            eT2 = [r1(29, 512, F32), r1(31, 512, F32), r1(33, 512, F32), r1(35, 512, F32)]
            Bc2 = [r1(37, 512, F32), r1(39, 512, F32)]
            eb2 = [r1(41, 512, F32), r1(43, 512, F32)]
            enb2 = [r1(45, 512, F32), r1(47, 512, F32)]
            ATs2 = [r1(49, 512, BF16), r1(50, 512, BF16)]
            on2 = [r1(51, 1024, BF16), r1(53, 1024, BF16)]
            Stmp, st_ids = r1(55, 1024, F32)
            qaT = qaT.rearrange("p (h t) -> p h t", t=512)
            kaT = kaT.rearrange("p (h t) -> p h t", t=512)
            katok = katok.rearrange("p (b c) -> p b c", c=512)
            silur = silur.rearrange("p (k t) -> p k t", t=512)

            slot_q, wid_q = ring_acquire("qa")
            slot_k, wid_k = None, None
            qbanks = []
            lnq = []
            for h in range(H_A):
                eT, e_ids = eT2[h]
                bk = next_bank(PB)
                ps = bank(bk)
                P.add("pe", (lambda e, ps=ps, h=h: e.matmul(ps[:, 0:TT], lhsT=wa2[0:16, h * 128:(h + 1) * 128], rhs=alrT[0:16, 0:TT],
                                                           start=True, stop=True)),
                      reads=al_ids + C2, writes=[pid(bk)], cost=0.45)
                P.add("act", (lambda e, ps=ps, h=h, eT=eT: e.activation(out=eT[:, 0:TT], in_=ps[:, 0:TT], func=AF.Exp, bias=negb[:, h:h + 1], scale=-1.0)),
                      reads=[pid(bk)] + C2, writes=e_ids + [("phA", par)])
            for h in range(H_A):
                eT, e_ids = eT2[h]
                P.add("act", (lambda e, eT=eT: e.activation(out=eT[:, 0:TT], in_=eT[:, 0:TT], func=AF.Ln, bias=1.0, scale=1.0)),
                      reads=e_ids + [("phA", par)], writes=e_ids + [("phB", par)])
            for h in range(H_A):
                eT, e_ids = eT2[h]
                Bc, B_ids = Bc2[h % 2]
                eb, ebh_ids = eb2[h % 2]
                enb, enb_ids = enb2[h % 2]
                P.add("dve", (lambda e, eT=eT, Bc=Bc: e.tensor_tensor_scan(out=Bc[:, 0:TT], data0=smask[:, 0:TT], data1=eT[:, 0:TT], initial=0.0,
                                                             op0=ALU.mult, op1=ALU.add)),
                      reads=e_ids + C2, writes=B_ids, cost=1.3)
                P.add("act", (lambda e, Bc=Bc, eb=eb: e.activation(out=eb[:, 0:TT], in_=Bc[:, 0:TT], func=AF.Exp, scale=-1.0 / 16.0)),
                      reads=B_ids + [("phB", par)], writes=ebh_ids)
                P.add("pool", (lambda e, h=h, eb=eb: e.tensor_copy(out=eblt[:, h, 0:nb], in_=eb[:, 0:TT].rearrange("p (b t) -> p b t", t=bt)[:, :, bt - 1])),
                      reads=ebh_ids, writes=["eblt"], cost=0.2)
                P.add("act", (lambda e, Bc=Bc, enb=enb: e.activation(out=enb[:, 0:TT], in_=Bc[:, 0:TT], func=AF.Exp, scale=1.0 / 16.0)),
                      reads=B_ids + [("phB", par)], writes=enb_ids)
                bkq, psq = proj_F(slot_q, wid_q, 0, 512, 8, hn_rhs, HN_ALL, h * 128, PB)
                P.add("dve", (lambda e, psq=psq, h=h, eb=eb: e.scalar_tensor_tensor(out=qaT[:, h, 0:TT], in0=psq[:, 0:TT], scalar=float(DK) ** -0.5,
                                                                            in1=eb[:, 0:TT], op0=ALU.mult, op1=ALU.mult)),
                      reads=[pid(bkq)] + ebh_ids, writes=[("R1", h)], cost=0.85)
                if h == H_A - 1:
                    ring_release()
                if h == 0:
                    pass
                if slot_k is None:
                    slot_k, wid_k = ring_acquire("ka")
                bkk, psk = proj_F(slot_k, wid_k, 0, 512, 8, hn_rhs, HN_ALL, h * 128, PB)
                P.add("dve", (lambda e, psk=psk, h=h, enb=enb: e.tensor_tensor(out=kaT[:, h, 0:TT], in0=psk[:, 0:TT], in1=enb[:, 0:TT], op=ALU.mult)),
                      reads=[pid(bkk)] + enb_ids, writes=[("R1", 4 + h)], cost=0.8)
            ring_release()
            for half in range(2):
                slot, wid = ring_acquire(f"ra{half}")
                for c in range(4):
                    bk, ps = proj_F(slot, wid, 0, 512, 8, hn_rhs, HN_ALL, c * 128, PB)
                    ch = half * 4 + c
                    th, th_ids = eT2[c % 2]
                    P.add("act", (lambda e, ps=ps, th=th: e.activation(out=th[:, 0:TT], in_=ps[:, 0:TT], func=AF.Tanh, scale=0.5)),
                          reads=[pid(bk)], writes=th_ids)
                    P.add("dve", (lambda e, ps=ps, ch=ch, th=th: e.scalar_tensor_tensor(out=silur[:, ch, 0:TT], in0=th[:, 0:TT], scalar=1.0, in1=ps[:, 0:TT],
                                                                                    op0=ALU.add, op1=ALU.mult)),
                          reads=[pid(bk)] + th_ids, writes=[("R1", 20 + ch)], cost=0.8)
                ring_release()

            qbT, qb_ids = r1(59, 4 * 512, BF16)
            PT, pt_ids = r1(0, 5 * 2 * 512, BF16)
            rs, rs_ids = r1(10, 512, F32)
            kfp, kf_ids = r1(39, 512, F32)
            vfp, vf_ids = r1(41, 512, F32)
            qbT = qbT.rearrange("p (c t) -> p c t", t=512)
            PT = PT.rearrange("p (k r q) -> p k r q", r=2, q=512)
            ch_, ph_ = tc["cur_half"], tc["prev_half"]
            KB_CUR = ("kbT", ch_)
            M2B = [6, 7]
            m2_steps = []
            m2s = {}

            def st_qb(c):
                def f():
                    if c == 0:
                        m2s["qb"] = ring_acquire("qb")
                    slot, wid = m2s["qb"]
                    bk, ps = proj_F(slot, wid, 0, 512, 8, hn_rhs, HN_ALL, c * 128, M2B)
                    P.add("act", (lambda e, ps=ps, c=c: e.activation(out=qbT[:, c, 0:TT], in_=ps[:, 0:TT], func=AF.Copy, scale=float(HD) ** -0.5)),
                          reads=[pid(bk)], writes=qb_ids)
                    if c == 3:
                        ring_release()
                return f

            def st_kb(c):
                def f():
                    if c == 0:
                        m2s["kb"] = ring_acquire("kb")
                    slot, wid = m2s["kb"]
                    bk, ps = proj_F(slot, wid, 0, 512, 8, hn_rhs, HN_ALL, c * 128, M2B)
                    P.add("dve", (lambda e, ps=ps, c=c: e.tensor_copy(out=kbT[:, c, ch_ * 512:ch_ * 512 + TT], in_=ps[:, 0:TT])),
                          reads=[pid(bk)], writes=[KB_CUR])
                    if c == 3:
                        if tc["kv_out"] is not None:
                            ko, vo = tc["kv_out"]
                            for b in range(nb):
                                bk, ps = proj_T(slot, wid, 0, 512, 8, hn_lhs, [hid(b)], b, 512, M2B)
                                P.add("dve", (lambda e, ps=ps: e.tensor_copy(out=kfp[0:bt, :], in_=ps[0:bt, :])), reads=[pid(bk)], writes=kf_ids)
                                P.add("pool", (lambda e, b=b, ko=ko: e.dma_start(out=ko[b * bt:(b + 1) * bt, :], in_=kfp[0:bt, :])),
                                      reads=kf_ids, writes=[("out", "k")], dsem=s_ko)
                        ring_release()
                return f

            def st_vb(b):
                def f():
                    if b == 0:
                        m2s["vb"] = ring_acquire("vb")
                    slot, wid = m2s["vb"]
                    bk, ps = proj_T(slot, wid, 0, 512, 8, hn_lhs, [hid(b)], b, 512, M2B)
                    vs_ = tc["vb_cur"][b]
                    P.add("act", (lambda e, ps=ps, vs_=vs_: e.activation(out=vbr[0:bt, vs_, :].rearrange("p (h e) -> p h e", e=65)[:, :, 0:64],
                                                                         in_=ps[0:bt, :].rearrange("p (h e) -> p h e", e=64), func=AF.Copy)),
                          reads=[pid(bk)], writes=[("vb", vs_)])
                    if tc["kv_out"] is not None:
                        ko, vo = tc["kv_out"]
                        P.add("dve", (lambda e, ps=ps: e.tensor_copy(out=vfp[0:bt, :], in_=ps[0:bt, :])), reads=[pid(bk)], writes=vf_ids)
                        P.add("pool", (lambda e, b=b, vo=vo: e.dma_start(out=vo[b * bt:(b + 1) * bt, :], in_=vfp[0:bt, :])),
                              reads=vf_ids, writes=[("out", "v")], dsem=s_vo)
                    if b == nb - 1:
                        ring_release()
                return f
            m2_steps = [st_qb(c) for c in range(4)] + [st_kb(c) for c in range(4)] + [st_vb(b) for b in range(nb)]

            for b in range(nb):
                c0 = b * bt
                ATs, at_ids = ATs2[b % 2]
                on_, on_ids = on2[b % 2]
                pvk = bankb(5).rearrange("p (h c) -> p h c", c=128)

                def trk(e, c0=c0, pvk=pvk):
                    ins = None
                    for h in range(H_A):
                        ins = e.transpose(out=pvk[0:bt, h, :], in_=kaT[:, h, c0:c0 + bt], identity=ident[:, :])
                    return ins
                P.add("pe", trk, reads=ka_ids + C2, writes=[pid(5)], cost=0.5)
                P.add("act", (lambda e, b=b, pvk=pvk: e.activation(out=katok[0:bt, b, :].rearrange("p (h c) -> p h c", c=128), in_=pvk[0:bt, 0:4, :], func=AF.Copy)),
                      reads=[pid(5)], writes=[("R1", 8 + b)], cost=0.6)
                psA = bank(4).rearrange("p (h i) -> p h i", i=128)

                def at_fn(e, c0=c0, psA=psA):
                    ins = None
                    for h in range(H_A):
                        ins = e.matmul(psA[0:bt, h, 0:bt], lhsT=kaT[:, h, c0:c0 + bt], rhs=qaT[:, h, c0:c0 + bt], start=True, stop=True)
                    return ins
                P.add("pe", at_fn, reads=ka_ids + qa_ids, writes=[pid(4)], cost=0.5)
                ATv = ATs.rearrange("p (h i) -> p h i", i=128)
                trib = bc_mid(tri[0:bt, 0:bt], 4)
                P.add("dve", (lambda e, psA=psA, ATv=ATv, trib=trib: e.tensor_tensor(out=ATv[0:bt, :, 0:bt], in0=psA[0:bt, :, 0:bt], in1=trib, op=ALU.mult)),
                      reads=[pid(4)] + C2, writes=at_ids, cost=0.75)
                pso = pd[0]

                def o_fn(e, b=b, c0=c0, ATv=ATv, pso=pso):
                    ins = None
                    for h in range(H_A):
                        e.matmul(pso[0:bt, h * 256:(h + 1) * 256], lhsT=ATv[0:bt, h, 0:bt], rhs=va[0:bt, b, h * 256:(h + 1) * 256], start=True, stop=False)
                        ins = e.matmul(pso[0:bt, h * 256:(h + 1) * 256], lhsT=qaT[:, h, c0:c0 + bt], rhs=S_b[:, h, :], start=False, stop=True)
                    return ins
                VA = lambda b: [("R1", 12 + 2 * b), ("R1", 13 + 2 * b)]
                P.add("pe", o_fn, reads=at_ids + VA(b) + qa_ids + ["S_b"], writes=[pid(0), pid(1)], cost=1.7)
                psS = pd[1]

                def s_fn(e, b=b, psS=psS):
                    ins = None
                    for h in range(H_A):
                        ins = e.matmul(psS[:, h * 256:(h + 1) * 256], lhsT=katok[0:bt, b, h * 128:(h + 1) * 128], rhs=va[0:bt, b, h * 256:(h + 1) * 256],
                                       start=True, stop=True)
                    return ins
                P.add("pe", s_fn, reads=[("R1", 8 + b)] + VA(b), writes=[pid(2), pid(3)], cost=0.85)
                P.add("dve", (lambda e, psS=psS: e.tensor_tensor(out=Stmp, in0=psS[:, :], in1=S_f[:].rearrange("p h v -> p (h v)"), op=ALU.add)),
                      reads=[pid(2), pid(3), "S_f"], writes=st_ids, cost=1.3)
                eblb = bc_last(eblt[:, :, b:b + 1], DV)
                P.add("dve", (lambda e, eblb=eblb: e.tensor_tensor(out=S_f[:], in0=Stmp.rearrange("p (h v) -> p h v", v=DV), in1=eblb, op=ALU.mult)),
                      reads=st_ids + ["eblt"], writes=["S_f"], cost=1.3)

                def scast(e, b=b):
                    ins = None
                    for h in range(H_A):
                        ins = e.activation(out=S_b[:, h, :], in_=Stmp[:, h * DV:(h + 1) * DV], func=AF.Copy, scale=eblt[:, h, b:b + 1])
                    return ins
                P.add("act", scast, reads=st_ids + ["eblt"], writes=["S_b"], cost=1.5)
                for h in range(H_A):
                    jb_, jid = jk()
                    P.add("act", (lambda e, h=h, pso=pso, jb_=jb_: e.activation(out=jb_[0:bt, 0:256], in_=pso[0:bt, h * 256:(h + 1) * 256], func=AF.Square,
                                                                              accum_out=stat[0:bt, 8 + h:9 + h])),
                          reads=[pid(0), pid(1)], writes=[SID(2), jid], cost=0.4)
                small_rstd(stat[0:bt, 8:12], 4, 1.0 / DV, [SID(2)], [SID(3)], stat[0:bt, 12:16])
                for h in range(H_A):
                    P.add("dve", (lambda e, h=h, pso=pso, on_=on_: e.scalar_tensor_tensor(out=on_[0:bt, h * 256:(h + 1) * 256], in0=pso[0:bt, h * 256:(h + 1) * 256],
                                                                                scalar=stat[0:bt, 12 + h:13 + h], in1=ggla[0:bt, h * 256:(h + 1) * 256],
                                                                                op0=ALU.mult, op1=ALU.mult)),
                          reads=[pid(0), pid(1), SID(3)] + C2, writes=on_ids, cost=0.45)
                bk = 4
                pv = bankb(bk).rearrange("p (k t) -> p k t", t=128)

                def tro(e, pv=pv, on_=on_):
                    ins = None
                    for kc in range(8):
                        ins = e.transpose(out=pv[:, kc, 0:bt], in_=on_[0:bt, kc * 128:(kc + 1) * 128], identity=ident[0:bt, 0:bt])
                    return ins
                P.add("pe", tro, reads=on_ids + C2, writes=[pid(bk)], cost=0.9)
                P.add("dve", (lambda e, pv=pv, c0=c0: e.tensor_tensor(out=oaT[:, :, c0:c0 + bt], in0=pv[:, :, 0:bt], in1=silur[:, :, c0:c0 + bt], op=ALU.mult)),
                      reads=[pid(bk)] + sr_ids, writes=[("oaT", b)], cost=1.2)
                nst = (len(m2_steps) + nb - 1 - b) // (nb - b) if nb - b > 0 else len(m2_steps)
                for _ in range(min(nst, len(m2_steps))):
                    m2_steps.pop(0)()
            while m2_steps:
                m2_steps.pop(0)()

            for qb in range(nb):
                q0 = qb * bt
                nq = bt
                kbl = []
                for kb in range(5):
                    idx = qb + kb if nb == 4 else kb
                    if idx < 4:
                        if not tc["has_prev"]:
                            continue
                        kbl.append((kb, 128, ph_ * 512 + idx * 128, tc["vb_prev"][idx], ("kbT", ph_)))
                    else:
                        cb_ = idx - 4
                        kbl.append((kb, bt, ch_ * 512 + cb_ * bt, tc["vb_cur"][cb_], KB_CUR))
                for (kb, nk, kc0, vslot, kid) in kbl:
                    for par in range(2):
                        bk = next_bank(PB)
                        ps = bank(bk)
                        psv = ps.rearrange("p (c q) -> p c q", q=128)
                        bT = biasT[0:nk, kb, par, :].rearrange("p (c q) -> p c q", q=128)

                        def sc_fn(e, ps=ps, psv=psv, bT=bT, nk=nk, kc0=kc0, par=par, q0=q0, nq=nq, kb_=kb):
                            ins = None
                            if nq == 128:
                                nob = kb_ in (1, 2)
                                if not nob:
                                    e.matmul(ps[0:nk, :], lhsT=ident[0:nk, 0:nk], rhs=biasT[0:nk, kb_, par, :], start=True, stop=False)
                                for c in range(4):
                                    ins = e.matmul(psv[0:nk, c, 0:nq], lhsT=kbT[par * 64:(par + 1) * 64, c, kc0:kc0 + nk],
                                                   rhs=qbT[par * 64:(par + 1) * 64, c, q0:q0 + nq], start=nob, stop=(nob or c == 3))
                            else:
                                for c in range(4):
                                    e.matmul(psv[0:nk, c, 0:nq], lhsT=ident[0:nk, 0:nk], rhs=bT[:, c, 0:nq], start=True, stop=False)
                                    ins = e.matmul(psv[0:nk, c, 0:nq], lhsT=kbT[par * 64:(par + 1) * 64, c, kc0:kc0 + nk],
                                                   rhs=qbT[par * 64:(par + 1) * 64, c, q0:q0 + nq], start=False, stop=True)
                            return ins
                        P.add("pe", sc_fn, reads=[kid] + qb_ids + C2, writes=[pid(bk)], cost=0.55)
                        PTv = PT[:, kb, par, :].rearrange("p (c q) -> p c q", q=128)
                        P.add("act", (lambda e, psv=psv, PTv=PTv, nk=nk, nq=nq: e.activation(out=PTv[0:nk, :, 0:nq], in_=psv[0:nk, :, 0:nq], func=AF.Exp)),
                              reads=[pid(bk)], writes=[("R1", kb * 2 + par)])
                pso2 = pd[2]

                def pv_fn(e, kbl=kbl, pso2=pso2, nq=nq):
                    ins = None
                    for h in range(H_B):
                        c, par = h // 2, h % 2
                        o0 = (h // 4) * 512 + (h % 4) * 65
                        for i, (kb, nk, kc0, vslot, kid) in enumerate(kbl):
                            PTv = PT[:, kb, par, :].rearrange("p (c q) -> p c q", q=128)
                            ins = e.matmul(pso2[0:nq, o0:o0 + 65], lhsT=PTv[0:nk, c, 0:nq], rhs=vbr[0:nk, vslot, h * 65:(h + 1) * 65],
                                           start=(i == 0), stop=(i == len(kbl) - 1))
                    return ins
                P.add("pe", pv_fn, reads=pt_ids + [("vb", k[3]) for k in kbl] + C2, writes=[pid(4), pid(5)], cost=0.45 * len(kbl))
                otok, ot_ids = r1(10, 512, BF16)
                for g in range(2):
                    pg = pso2[0:nq, g * 512:g * 512 + 260].rearrange("p (h e) -> p h e", e=65)
                    rv = stat[0:nq, 32 + 4 * g:36 + 4 * g]
                    P.add("dve", (lambda e, pg=pg, rv=rv: e.reciprocal(out=rv, in_=pg[:, :, 64])), reads=[pid(4 + g)], writes=[SID((7, g))], cost=0.2)
                    rvb = bc_new(rv, 64)
                    ov = otok[0:nq, g * 256:(g + 1) * 256].rearrange("p (h e) -> p h e", e=64)
                    P.add("dve", (lambda e, pg=pg, rvb=rvb, ov=ov: e.tensor_tensor(out=ov, in0=pg[:, :, 0:64], in1=rvb, op=ALU.mult)),
                          reads=[pid(4 + g), SID((7, g))], writes=ot_ids, cost=0.45)
                bk = next_bank(trb)
                pv = bankb(bk).rearrange("p (c t) -> p c t", t=128)

                def tra(e, pv=pv, nq=nq):
                    ins = None
                    for c in range(4):
                        ins = e.transpose(out=pv[:, c, 0:nq], in_=otok[0:nq, c * 128:(c + 1) * 128], identity=ident[0:nq, 0:nq])
                    return ins
                P.add("pe", tra, reads=ot_ids + C2, writes=[pid(bk)], cost=0.5)
                P.add("act", (lambda e, pv=pv, nq=nq, q0=q0: e.activation(out=obT[:, :, q0:q0 + nq], in_=pv[:, 0:4, 0:nq], func=AF.Copy)),
                      reads=[pid(bk)], writes=[("obT", qb)], cost=0.6)

            sga2 = [r1(12, 512, F32), r1(14, 512, F32)]
            sgb2 = [r1(16, 512, F32), r1(18, 512, F32)]
            t12 = [r1(20, 512, F32), r1(22, 512, F32)]
            t22 = [r1(24, 512, F32), r1(26, 512, F32)]
            mixT, mx_ids = r1(28, 8 * 512, BF16)
            mixT = mixT.rearrange("p (k t) -> p k t", t=512)
            OA_ALL = [("oaT", b) for b in range(nb)]
            OB_ALL = [("obT", b) for b in range(nb)]
            AB = [0, 1, 2, 3, 4, 5]
            for g in range(4):
                slot_a, wid_a = ring_acquire(f"mixa{g}")
                slot_b, wid_b = ring_acquire(f"mixb{g}")
                for jj in range(2):
                    ch = g * 2 + jj
                    sga, sga_ids = sga2[jj]
                    sgb, sgb_ids = sgb2[jj]
                    t1, t1_ids = t12[jj]
                    t2, t2_ids = t22[jj]
                    bk, ps = proj_F(slot_a, wid_a, 0, 512, 8, hn_rhs, HN_ALL, jj * 128, AB)
                    P.add("act", (lambda e, ps=ps, sga=sga: e.activation(out=sga[:, 0:TT], in_=ps[:, 0:TT], func=AF.Tanh, scale=0.5)), reads=[pid(bk)], writes=sga_ids)
                    bk, ps = proj_F(slot_a, wid_a, 256, 512, 8, (lambda kc: oaT[:, kc, 0:TT]), OA_ALL, jj * 128, AB)
                    P.add("dve", (lambda e, ps=ps, t1=t1, sga=sga: e.scalar_tensor_tensor(out=t1[:, 0:TT], in0=sga[:, 0:TT], scalar=1.0, in1=ps[:, 0:TT], op0=ALU.add, op1=ALU.mult)),
                          reads=[pid(bk)] + sga_ids, writes=t1_ids)
                    bk, ps = proj_F(slot_b, wid_b, 0, 256, 8, hn_rhs, HN_ALL, jj * 128, AB)
                    P.add("act", (lambda e, ps=ps, sgb=sgb: e.activation(out=sgb[:, 0:TT], in_=ps[:, 0:TT], func=AF.Tanh, scale=0.5)), reads=[pid(bk)], writes=sgb_ids)
                    bk, ps = proj_F(slot_b, wid_b, 2048, 256, 4, (lambda kc: obT[:, kc, 0:TT]), OB_ALL, jj * 128, AB)
                    P.add("dve", (lambda e, ps=ps, t2=t2, sgb=sgb: e.scalar_tensor_tensor(out=t2[:, 0:TT], in0=sgb[:, 0:TT], scalar=1.0, in1=ps[:, 0:TT], op0=ALU.add, op1=ALU.mult)),
                          reads=[pid(bk)] + sgb_ids, writes=t2_ids)
                    P.add("dve", (lambda e, ch=ch, t1=t1, t2=t2: e.scalar_tensor_tensor(out=mixT[:, ch, 0:TT], in0=t1[:, 0:TT], scalar=0.5, in1=t2[:, 0:TT], op0=ALU.mult, op1=ALU.add)),
                          reads=t1_ids + t2_ids, writes=[("R1", 28 + ch)], cost=0.8)
                ring_release()
                ring_release()

            msb, m_ids = r1(36, 4 * 1024, F32)
            msb = msb.rearrange("p (b c) -> p b c", c=1024)
            MB = lambda b: [("R1", 36 + 4 * b + i) for i in range(4)]

            def postnorm(gi, from_ps=None, blks=None):
                for b in (range(nb) if blks is None else blks):
                    P.add("dve", (lambda e, b=b: e.tensor_tensor(out=stat[0:bt, 24 + b:25 + b], in0=stat[0:bt, 16 + 2 * b:17 + 2 * b],
                                                                 in1=stat[0:bt, 17 + 2 * b:18 + 2 * b], op=ALU.add)),
                          reads=[SID((4, b))], writes=[SID((5, b))], cost=0.15)
                    small_rstd(stat[0:bt, 24 + b:25 + b], 1, 1.0 / D, [SID((5, b))], [SID((6, b))], stat[0:bt, 28 + b:29 + b])
                    if from_ps is None:
                        P.add("dve", (lambda e, b=b: e.scalar_tensor_tensor(out=msb[0:bt, b, :], in0=msb[0:bt, b, :], scalar=stat[0:bt, 28 + b:29 + b],
                                                                            in1=gpost[0:bt, gi, :], op0=ALU.mult, op1=ALU.mult)),
                              reads=MB(b) + [SID((6, b))] + C2, writes=MB(b), cost=1.3)
                    else:
                        for half in range(2):
                            ps, bk = from_ps[(b, half)]
                            P.add("dve", (lambda e, b=b, half=half, ps=ps: e.scalar_tensor_tensor(
                                out=msb[0:bt, b, half * 512:(half + 1) * 512], in0=ps[0:bt, :], scalar=stat[0:bt, 28 + b:29 + b],
                                in1=gpost[0:bt, gi, half * 512:(half + 1) * 512], op0=ALU.mult, op1=ALU.mult)),
                                reads=[pid(bk), SID((6, b))] + C2, writes=MB(b), cost=0.7)
                    P.add("dve", (lambda e, b=b: e.tensor_tensor(out=xb[0:bt, b, :], in0=xb[0:bt, b, :], in1=msb[0:bt, b, :], op=ALU.add)),
                          reads=MB(b) + [xid(b)], writes=[xid(b)], cost=1.25)

            def evac_T(ps, bk, b, half, sqs=1.0, copy=True):
                jb_, jid = jk()
                P.add("act", (lambda e, ps=ps, b=b, half=half, jb_=jb_: e.activation(out=jb_[0:bt, 0:512], in_=ps[0:bt, :], func=AF.Square, scale=sqs,
                                                                          accum_out=stat[0:bt, 16 + 2 * b + half:17 + 2 * b + half])),
                      reads=[pid(bk)], writes=[SID((4, b)), jid])
                if copy:
                    P.add("dve", (lambda e, ps=ps, b=b, half=half: e.tensor_copy(out=msb[0:bt, b, half * 512:(half + 1) * 512], in_=ps[0:bt, :])),
                          reads=[pid(bk)], writes=MB(b), cost=0.6)

            wo = [ring_acquire("wo0"), ring_acquire("wo1")]
            wo_ps = {}
            for b in range(nb):
                for half in range(2):
                    slot, wid = wo[half]
                    bk, ps = proj_T(slot, wid, 0, 512, 8, (lambda kc, b: mixT[:, kc, b * bt:(b + 1) * bt]), mx_ids, b, 512, [0, 1, 2, 3, 4, 5])
                    evac_T(ps, bk, b, half, 0.5, copy=False)
                    wo_ps[(b, half)] = (ps, bk)
                postnorm(0, wo_ps, [b])
            ring_release()
            ring_release()

            prenorm(1)
            uT, u_ids = r1(0, NFF * 512, BF16)
            uT = uT.rearrange("p (j t) -> p j t", t=512)
            G2 = [r1(22, 640, F32), r1(25, 640, F32)]
            c12 = [r1(28, 512, F32), r1(30, 512, F32)]
            c22 = [r1(32, 512, F32), r1(34, 512, F32)]
            ge2 = [r1(52, 512, F32), r1(54, 512, F32)]
            FB = [0, 1, 2, 3, 4, 5]
            for jb in range(NFF // 2):
                slot, wid = ring_acquire(f"up{jb}")
                for jj in range(2):
                    j = jb * 2 + jj
                    G, g_ids = G2[jj]
                    c1, c1_ids = c12[jj]
                    c2, c2_ids = c22[jj]
                    ge, ge_ids = ge2[jj]
                    bka, psa = proj_F(slot, wid, jj * 256, 512, 8, hn_rhs, HN_ALL, 0, FB)
                    bkg, psg = proj_F(slot, wid, jj * 256 + 128, 512, 8, hn_rhs, HN_ALL, 0, FB)
                    P.add("pool", (lambda e, j=j, G=G: e.tensor_copy(out=G[:, 0:2], in_=carry[:, j, :])), reads=[("carry", j)], writes=g_ids, cost=0.2)
                    P.add("act", (lambda e, psg=psg, G=G: e.activation(out=G[:, 2:2 + TT], in_=psg[:, 0:TT], func=AF.Copy)), reads=[pid(bkg)], writes=g_ids, cost=0.66)
                    P.add("pool", (lambda e, j=j, G=G: e.tensor_copy(out=carry[:, j, :], in_=G[:, TT:TT + 2])), reads=g_ids, writes=[("carry", j)], cost=0.2)
                    P.add("dve", (lambda e, j=j, G=G, c1=c1: e.tensor_scalar(out=c1[:, 0:TT], in0=G[:, 2:2 + TT], scalar1=cw[:, 2, j:j + 1], scalar2=cb[:, j:j + 1],
                                                                 op0=ALU.mult, op1=ALU.add)),
                          reads=g_ids + C2, writes=c1_ids, cost=0.7)
                    P.add("dve", (lambda e, j=j, G=G, c1=c1, c2=c2: e.scalar_tensor_tensor(out=c2[:, 0:TT], in0=G[:, 1:1 + TT], scalar=cw[:, 1, j:j + 1], in1=c1[:, 0:TT],
                                                                        op0=ALU.mult, op1=ALU.add)),
                          reads=g_ids + c1_ids + C2, writes=c2_ids, cost=0.7)
                    P.add("dve", (lambda e, j=j, G=G, c1=c1, c2=c2: e.scalar_tensor_tensor(out=c1[:, 0:TT], in0=G[:, 0:TT], scalar=cw[:, 0, j:j + 1], in1=c2[:, 0:TT],
                                                                        op0=ALU.mult, op1=ALU.add)),
                          reads=g_ids + c2_ids + C2, writes=c1_ids, cost=0.7)
                    P.add("act", (lambda e, c1=c1, ge=ge: e.activation(out=ge[:, 0:TT], in_=c1[:, 0:TT], func=AF.Gelu_apprx_tanh)), reads=c1_ids, writes=ge_ids, cost=0.62)
                    P.add("dve", (lambda e, j=j, psa=psa, ge=ge: e.tensor_tensor(out=uT[:, j, 0:TT], in0=psa[:, 0:TT], in1=ge[:, 0:TT], op=ALU.mult)),
                          reads=[pid(bka)] + ge_ids, writes=[("R1", j)], cost=0.78)
                ring_release()
            for half in range(2):
                banks = [0, 1, 2, 3] if half == 0 else [4, 5, 6, 7]
                for kbi, (k0, kn) in enumerate(((0, 8), (8, 8), (16, 6))):
                    slot, wid = ring_acquire(f"dn{half}_{kbi}")
                    urd = [("R1", kc) for kc in range(k0, k0 + kn)]
                    if kbi < 2:
                        def dn_fn(e, slot=slot, k0=k0, kn=kn, banks=banks):
                            ins = None
                            for kl in range(kn):
                                kc = k0 + kl
                                for b in range(nb):
                                    ins = e.matmul(bank(banks[b])[0:bt, :], lhsT=uT[:, kc, b * bt:(b + 1) * bt], rhs=slot[:, kl * 512:(kl + 1) * 512],
                                                   start=(kc == 0), stop=(kc == NFF - 1))
                            return ins
                        P.add("pe", dn_fn, reads=[wid] + urd, writes=[pid(banks[b]) for b in range(nb)], cost=0.25 * kn * nb)
                    else:
                        for b in range(nb):
                            def dn_fb(e, slot=slot, k0=k0, kn=kn, banks=banks, b=b):
                                ins = None
                                for kl in range(kn):
                                    kc = k0 + kl
                                    ins = e.matmul(bank(banks[b])[0:bt, :], lhsT=uT[:, kc, b * bt:(b + 1) * bt], rhs=slot[:, kl * 512:(kl + 1) * 512],
                                                   start=(kc == 0), stop=(kc == NFF - 1))
                                return ins
                            P.add("pe", dn_fb, reads=[wid] + urd, writes=[pid(banks[b])], cost=0.25 * kn)
                            evac_T(bank(banks[b]), banks[b], b, half)
                            if half == 1:
                                postnorm(1, blks=[b])
                    ring_release()

            yield "body"
            hnP, hnP_ids = r1(0, 8 * 512, BF16)
            hnP = hnP.rearrange("p (b k t) -> p b k t", k=8, t=128)
            HP = lambda b: [("R1", 2 * b), ("R1", 2 * b + 1)]
            prenorm(2, (lambda b: hnP[:, b, :, 0:bt]), HP)
            pbf, pb_ids = r1(8, 1024, BF16)
            pT, pT_ids = r1(10, 1024, BF16)
            sgp2 = [r1(52, 512, F32), r1(54, 512, F32)]
            pT = pT.rearrange("p (k t) -> p k t", t=512)
            pbf = pbf.rearrange("p (b c) -> p b c", c=256)
            P.add("pool", (lambda e: e.tensor_copy(out=pbf[0:bt, 0:nb, :], in_=pbuf[0:bt, 0:nb, :])), reads=["pbuf"], writes=pb_ids, cost=1.9)
            for b in range(nb):
                bk = next_bank(trb)
                pv = bankb(bk).rearrange("p (k t) -> p k t", t=128)

                def trp(e, b=b, pv=pv):
                    ins = None
                    for kc in range(2):
                        ins = e.transpose(out=pv[:, kc, 0:bt], in_=pbf[0:bt, b, kc * 128:(kc + 1) * 128], identity=ident[0:bt, 0:bt])
                    return ins
                P.add("pe", trp, reads=pb_ids + C2, writes=[pid(bk)], cost=0.25)
                P.add("dve", (lambda e, b=b, pv=pv: e.tensor_copy(out=pT[:, :, b * bt:(b + 1) * bt], in_=pv[:, 0:2, 0:bt])), reads=[pid(bk)], writes=pT_ids, cost=0.4)
            slots_g = [ring_acquire("pg0"), ring_acquire("pg1")]
            slot_l, wid_l = ring_acquire("pl")
            k_ = 0
            for b in range(nb):
                for half in range(2):
                    slot, wid = slots_g[half]
                    sgp, sgp_ids = sgp2[k_ % 2]
                    k_ += 1
                    bkg, psg = proj_T(slot, wid, 0, 512, 8, (lambda kc, b: hnP[:, b, kc, 0:bt]), HP(b), b, 512, [0, 1, 2, 3, 4, 5])
                    bkv, psv = proj_T(slot_l, wid_l, 0, 1024, 2, (lambda kc, b: pT[:, kc, b * bt:(b + 1) * bt]), pT_ids, b, 512, [0, 1, 2, 3, 4, 5], col0=half * 512)
                    P.add("act", (lambda e, psg=psg, sgp=sgp: e.activation(out=sgp[0:bt, :], in_=psg[0:bt, :], func=AF.Tanh, scale=0.5)), reads=[pid(bkg)], writes=sgp_ids)
                    P.add("dve", (lambda e, psv=psv, b=b, half=half, sgp=sgp: e.scalar_tensor_tensor(out=msb[0:bt, b, half * 512:(half + 1) * 512], in0=sgp[0:bt, :], scalar=1.0, in1=psv[0:bt, :], op0=ALU.add, op1=ALU.mult)),
                          reads=[pid(bkv)] + sgp_ids, writes=MB(b), cost=0.78)
                    jb_, jid = jk()
                    P.add("act", (lambda e, b=b, half=half, jb_=jb_: e.activation(out=jb_[0:bt, 0:512], in_=msb[0:bt, b, half * 512:(half + 1) * 512], func=AF.Square, scale=0.5,
                                                                       accum_out=stat[0:bt, 16 + 2 * b + half:17 + 2 * b + half])),
                          reads=MB(b), writes=[SID((4, b)), jid])
                postnorm(2, blks=[b])
            ring_release()
            ring_release()
            ring_release()
            ydst = tc["y_dst"]
            P.add("pool", (lambda e: e.dma_start(out=ydst, in_=xb[0:bt, 0:nb, :])), reads=[xid(b) for b in range(nb)],
                  writes=[("out", "y", xi)], dsem=s_y[xi])

        name2idx = {b["name"]: i for i, b in enumerate(WB)}
        HB = ["va0", "va1"]
        B2B = ["pg0", "pg1", "pl"]
        B1B = [b["name"] for b in WB if b["name"] not in HB + B2B]
        order = list(HB)
        for ti in range(nt):
            order += B1B
            if ti + 1 < nt:
                order += HB
            order += B2B
        order += HB + B1B + B2B
        seqblocks.extend(name2idx[n] for n in order)
        for _ in range(NSLOT):
            ring_issue()

        def load_x(ti):
            xi = ti % 2
            src = xp[ti * T:(ti + 1) * T, :].rearrange("(b p) d -> p b d", p=128)
            P.add("sp", (lambda e: e.dma_start(out=xbuf[xi][:], in_=src)), reads=[("out", "y", xi)],
                  writes=[("x", xi, b) for b in range(4)], dsem=s_x[xi])

        def load_xs():
            xi = nt % 2
            P.add("sp", (lambda e, xi=xi: e.dma_start(out=xbuf[xi][0:TS, 0, :], in_=xs)), reads=[("out", "y", xi)],
                  writes=[("x", xi, 0)], dsem=s_x[xi])

        def mk_tc(ti):
            last = (ti == nt - 1)
            return dict(nb=4, bt=128, par=ti % 2, xb=xbuf[ti % 2], xi=ti % 2, cur_half=ti % 2, prev_half=1 - ti % 2,
                        vb_cur=[(4 * ti + b) % 8 for b in range(4)], vb_prev=[(4 * (ti - 1) + b) % 8 for b in range(4)],
                        has_prev=(ti > 0), kv_out=((kp, vp) if last else None),
                        p_src=pp[ti * T:(ti + 1) * T, :].rearrange("(b p) d -> p b d", p=128),
                        y_dst=yp[ti * T:(ti + 1) * T, :].rearrange("(b p) d -> p b d", p=128))

        load_x(0)
        if nt > 1:
            load_x(1)
        else:
            load_xs()
        gens = {0: emit_tile(mk_tc(0))}
        next(gens[0])
        for ti in range(nt):
            next(gens[ti])
            if ti + 1 < nt:
                gens[ti + 1] = emit_tile(mk_tc(ti + 1))
                next(gens[ti + 1])
            for _ in gens[ti]:
                pass
            if ti + 2 < nt:
                load_x(ti + 2)
            elif ti + 2 == nt:
                load_xs()

        P.add("pool", (lambda e: e.dma_start(out=spo.rearrange("h c v -> c h v"), in_=S_f[:])), reads=["S_f"], writes=[("out", "sp")], dsem=s_o[0])

        def conv_store(dst, key, sems2):
            for r in range(2):
                def fn(e, r=r):
                    with nc.allow_non_contiguous_dma(reason="tiny conv state"):
                        return e.dma_start(out=dst[r, :].rearrange("(j p) -> p j", p=128), in_=carry[:, :, r])
                P.add("pool", fn, reads=[("carry", j) for j in range(NFF)], writes=[("out", key, r), ("carryall",)], dsem=sems2[r])
        conv_store(cpo, "cp", s_c[0:2])
        P.add("sp", (lambda e: e.dma_start(out=S_f[:], in_=sg.rearrange("h c v -> c h v"))), reads=[("out", "sp")], writes=["S_f"], dsem=s_l[0])
        P.add("act", (lambda e: e.activation(out=S_b[:], in_=S_f[:], func=AF.Copy)), reads=["S_f"], writes=["S_b"])

        for r in range(2):
            def cld(e, r=r):
                with nc.allow_non_contiguous_dma(reason="tiny conv state"):
                    return e.dma_start(out=carry[:, :, r], in_=sc[r, :].rearrange("(j p) -> p j", p=128))
            P.add("pool", cld, reads=[("carryall",)], writes=[("carry", j) for j in range(NFF)] + [("carryall",)], dsem=s_c[2 + r])
        ckf, ckf_ids = r1(36, 4 * 512, F32)
        cvf, cvf_ids = r1(44, 4 * 512, F32)
        ckb, ckb_ids = r1(52, 4 * 512 // 2, BF16)
        ckf = ckf.rearrange("p (b c) -> p b c", c=512)
        cvf = cvf.rearrange("p (b c) -> p b c", c=512)
        P.add("sp", (lambda e: e.dma_start(out=ckf, in_=ck.rearrange("(b p) c -> p b c", p=128))), writes=ckf_ids, dsem=s_l[1])
        P.add("sp", (lambda e: e.dma_start(out=cvf, in_=cv.rearrange("(b p) c -> p b c", p=128))), writes=cvf_ids, dsem=s_l[2])
        ckb2 = ckb.rearrange("p (b c) -> p b c", c=512)
        for b2 in range(2):
            P.add("dve", (lambda e, b2=b2: e.tensor_copy(out=ckb2[:, :, :], in_=ckf[:, 2 * b2:2 * b2 + 2, :])), reads=ckf_ids, writes=ckb_ids)
            for bb in range(2):
                b = 2 * b2 + bb
                bk = 6 + bb
                pv = bankb(bk).rearrange("p (c t) -> p c t", t=128)

                def trc(e, bb=bb, pv=pv):
                    ins = None
                    for c in range(4):
                        ins = e.transpose(out=pv[:, c, :], in_=ckb2[:, bb, c * 128:(c + 1) * 128], identity=ident[:, :])
                    return ins
                P.add("pe", trc, reads=ckb_ids + C2, writes=[pid(bk)])
                P.add("dve", (lambda e, b=b, pv=pv: e.tensor_copy(out=kbT[:, :, 512 + b * 128:512 + (b + 1) * 128], in_=pv[:, 0:4, :])),
                      reads=[pid(bk)], writes=[("kbT", 1)])
        for b in range(4):
            P.add("pool", (lambda e, b=b: e.tensor_copy(out=vbr[:, 4 + b, :].rearrange("p (h e) -> p h e", e=65)[:, :, 0:64],
                                                        in_=cvf[:, b, :].rearrange("p (h e) -> p h e", e=64))), reads=cvf_ids, writes=[("vb", 4 + b)])

        tcs = dict(nb=1, bt=TS, par=nt % 2, xb=xbuf[nt % 2], xi=nt % 2, cur_half=0, prev_half=1,
                   vb_cur=[0], vb_prev=[4, 5, 6, 7], has_prev=True, kv_out=(kso, vso),
                   p_src=pps.rearrange("(b p) d -> p b d", p=TS), y_dst=ys.rearrange("(b p) d -> p b d", p=TS))
        for _ in emit_tile(tcs):
            pass
        P.add("pool", (lambda e: e.dma_start(out=sso.rearrange("h c v -> c h v"), in_=S_f[:])), reads=["S_f"], writes=[("out", "ss")], dsem=s_o[3])
        conv_store(cso, "cs", s_c[4:6])
        P.add("sp", None, reads=[("out", "y", 0), ("out", "y", 1), ("out", "k"), ("out", "v"), ("out", "sp"), ("out", "cp", 0), ("out", "cp", 1),
                                 ("out", "ss"), ("out", "cs", 0), ("out", "cs", 1)])
        print("total ops", P.nadd)
        P.dbg = bool(_os.environ.get("KDBG"))
        if not _os.environ.get("KNOSCHED"):
            W_ = int(_os.environ.get("KW_PE", 24))
            P.schedule(dict(pe=W_, act=W_ * 2 // 3, dve=W_ * 2 // 3, pool=W_ * 2 // 3, sp=8))
            print("sched est time us", P.est_time)
        block = es.enter_context(nc.Block())
        P.emit(nc, block, sems)
    return nc


def _bias_tiles(rel_bias):
    kb = np.arange(5)[:, None, None]
    key = np.arange(128)[None, :, None]
    q = np.arange(128)[None, None, :]
    d = 512 + q - kb * 128 - key
    idx = np.clip(d, -128, 128) + 128
    kc_rel = (kb * 128 + key) // 64
    qc_rel = 8 + q // 64
    vis = (kc_rel <= qc_rel) & (kc_rel >= qc_rel - 8)
    vis = np.broadcast_to(vis, idx.shape)
    out = np.empty((5, 2, 128, 4, 128), np.float32)
    for par in range(2):
        for c in range(4):
            h = 2 * c + par
            vals = rel_bias[h][idx]
            out[:, par, :, c, :] = np.where(vis, vals, np.float32(NEG))
    return out.reshape(5, 2, 128, 512)


_NC_CACHE = {}


def kernel(x_prompt, x_sample, cache_attn_k, cache_attn_v, state_gla, state_conv,
           p_prompt, p_sample, g_pre_mix, w_in, w_a2, b_a2, g_gla, rel_bias,
           w_br_a, w_br_b, w_out, g_post_mix, g_pre_ffn, w_up, conv_w, conv_b,
           w_down, g_post_ffn, g_pre_ple, w_ple_gate, w_ple, g_post_ple):
    f = lambda a: np.ascontiguousarray(np.asarray(a, dtype=np.float32))
    x_prompt = f(x_prompt)
    seq = x_prompt.shape[1]
    nt = seq // T
    ncores = x_prompt.shape[0]
    if nt not in _NC_CACHE:
        _NC_CACHE[nt] = build(nt)
    nc = _NC_CACHE[nt]
    gvec = np.stack([f(g_pre_mix)[0], f(g_post_mix)[0], f(g_pre_ffn)[0], f(g_post_ffn)[0],
                     f(g_pre_ple)[0], f(g_post_ple)[0], f(g_gla)[0]], axis=0)
    cmat = np.zeros((128, 768), np.float32)
    cmat[:, 0:128] = np.eye(128, dtype=np.float32)
    cmat[:, 128:256] = np.triu(np.ones((128, 128), np.float32))
    sm = np.ones((512,), np.float32)
    sm[::128] = 0.0
    cmat[:, 256:768] = sm[None, :]
    shared = dict(w_in=f(w_in)[0], w_a2=f(w_a2)[0], w_br_a=f(w_br_a)[0], w_br_b=f(w_br_b)[0], w_out=f(w_out)[0],
                  w_up=f(w_up)[0], w_down=f(w_down)[0], w_ple_gate=f(w_ple_gate)[0], w_ple=f(w_ple)[0],
                  gvec=gvec, b_a2=f(b_a2)[0], conv_w=f(conv_w)[0], conv_b=f(conv_b)[0],
                  biasT=_bias_tiles(f(rel_bias)[0]), cmat=cmat)
    in_maps = []
    for c in range(ncores):
        m = dict(shared)
        m.update(xp=x_prompt[c], pp=f(p_prompt)[0, c], xs=f(x_sample)[c], pps=f(p_sample)[0, c],
                 ck=f(cache_attn_k)[0, c].reshape(512, 512), cv=f(cache_attn_v)[0, c].reshape(512, 512),
                 sg=f(state_gla)[0, c], sc=f(state_conv)[0, c])
        in_maps.append(m)
    res = run_bass_kernel_spmd(nc, in_maps, core_ids=list(range(ncores)))
    R = res.results
    st = lambda k: np.stack([np.asarray(r[k], dtype=np.float32) for r in R], axis=0)
    keep = min(512, seq)
    yp = st("yp")
    ys = st("ys")
    kpo = st("kp").reshape(ncores, keep, H_B, HD)[None]
    vpo = st("vp").reshape(ncores, keep, H_B, HD)[None]
    spo = st("spo")[None]
    cpo = st("cpo")[None]
    kso = st("kso").reshape(ncores, TS, H_B, HD)[None]
    vso = st("vso").reshape(ncores, TS, H_B, HD)[None]
    sso = st("sso")[None]
    cso = st("cso")[None]
    return (yp, ys, kpo, vpo, spo, cpo, kso, vso, sso, cso)
```

```python
import contextlib
import numpy as np
import concourse.bass as bass
import concourse.mybir as mybir
from concourse.bass_utils import run_bass_kernel_spmd

F32 = mybir.dt.float32
BF16 = mybir.dt.bfloat16
AF = mybir.ActivationFunctionType
ALU = mybir.AluOpType

D = 1024
SEQ = 8192
TS = 32
T = 512
H_A, DK, DV = 4, 128, 256
H_B, HD = 8, 64
DFF = 2816
NFF = 22
PLE = 256
EPS = 1e-6
NIN = 6672
C_QA, C_KA, C_VA, C_RA, C_ALR, C_QB, C_KB, C_VB, C_GA, C_GB = 0, 512, 1024, 2048, 3072, 3088, 3600, 4112, 4624, 5648
NSLOT = 3
SLOT_F = 4096
NEG = -30000.0


class Op:
    __slots__ = ("eng", "fn", "deps", "inc", "dsem", "dval", "group", "ev", "cost", "lat", "odeps", "line", "st")

    def __init__(self, eng, fn, dsem=None, group=False):
        self.eng, self.fn, self.dsem, self.group = eng, fn, dsem, group
        self.deps = []
        self.odeps = []
        self.cost = 0.0
        self.lat = 0.0
        self.inc = False
        self.dval = 0
        self.ev = None


class Prog:
    ENGS = ("pe", "act", "dve", "pool", "sp")

    def __init__(self):
        self.ops = {e: [] for e in self.ENGS}
        self.lastw = {}
        self.readers = {}
        self.dcount = {}
        self.nadd = 0
        self.dbg = False
        self.limit = None
        self.labels = []

    DEF_COST = dict(pe=2.0, act=0.65, dve=0.7, pool=1.0, sp=0.05)

    def add(self, eng, fn, reads=(), writes=(), dsem=None, group=False, cost=None, lat=None):
        op = Op(eng, fn, dsem, group)
        import sys as _s3
        fr = _s3._getframe(1)
        op.line = (fr.f_lineno, fr.f_back.f_lineno if fr.f_back else 0)
        if dsem is not None:
            op.cost = 0.05 if eng != "pool" else 0.3
            op.lat = 4.0 if lat is None else lat
        else:
            op.cost = self.DEF_COST[eng] if cost is None else cost
            op.lat = 0.15
        if fn is None:
            op.cost = 0.0
        self.nadd += 1
        import os as _os2, sys as _sys2
        if _os2.environ.get("KSHOW") and abs(self.nadd - int(_os2.environ["KSHOW"])) <= 2:
            fr = _sys2._getframe(1)
            print("OP", self.nadd, eng, "line", fr.f_lineno, "caller", fr.f_back.f_lineno if fr.f_back else None)
        if self.limit is not None and self.nadd > self.limit and fn is not None:
            return op
        psr = [b for b in reads if isinstance(b, tuple) and b[0] == "ps"]
        if psr:
            reads = [b for b in reads if not (isinstance(b, tuple) and b[0] == "ps")]
            writes = list(writes) + [b for b in psr if b not in writes]
        seen = set()
        for b in reads:
            w = self.lastw.get(b)
            if w is not None and id(w) not in seen:
                seen.add(id(w))
                op.deps.append(w)
        for b in writes:
            w = self.lastw.get(b)
            if w is not None and id(w) not in seen:
                if not (group and w.dsem is dsem):
                    seen.add(id(w))
                    op.deps.append(w)
            for r in self.readers.get(b, ()):
                if id(r) not in seen:
                    seen.add(id(r))
                    op.deps.append(r)
        op.odeps = list(op.deps)
        if eng == "pe":
            op.deps = [d for d in op.deps if not (d.dsem is None and d.eng == "pe")]
        for b in writes:
            self.lastw[b] = op
            self.readers[b] = []
        for b in reads:
            if isinstance(b, tuple) and b[0] == "const":
                continue
            self.readers.setdefault(b, []).append(op)
        if dsem is not None:
            k = id(dsem)
            self.dcount[k] = self.dcount.get(k, 0) + 16
            op.dval = self.dcount[k]
        self.ops[eng].append(op)
        if dsem is not None:
            self.dsems = getattr(self, "dsems", {})
            self.dsems[id(dsem)] = dsem
        return op

    def schedule(self, window):
        pending = {e: list(self.ops[e]) for e in self.ENGS}
        sched = {e: [] for e in self.ENGS}
        tnow = {e: 0.0 for e in self.ENGS}
        done = {}
        remaining = sum(len(v) for v in pending.values())
        while remaining:
            best = None
            for e in self.ENGS:
                pe_ = pending[e]
                te = tnow[e]
                for i in range(min(window[e], len(pe_))):
                    c = pe_[i]
                    rt = 0.0
                    ok = True
                    for d in c.odeps:
                        t = done.get(id(d))
                        if t is None:
                            ok = False
                            break
                        if t > rt:
                            rt = t
                    if not ok:
                        continue
                    st = rt if rt > te else te
                    key = (st, i)
                    if best is None or key < best[0]:
                        best = (key, e, i, c, st)
                    if rt <= te:
                        break
            if best is None:
                raise RuntimeError("scheduler deadlock")
            _, e, i, c, st = best
            c.st = st
            if self.dbg and e == "pe" and st > tnow[e] + 1.5:
                blk = max(c.odeps, key=lambda d: done[id(d)])
                print(f"PE gap {st - tnow[e]:5.1f}us at t={st:8.1f} op line {c.line} waits for {blk.eng} line {blk.line} (started {blk.st:.1f}, cost {blk.cost})")
            pending[e].pop(i)
            sched[e].append(c)
            tnow[e] = st + c.cost
            done[id(c)] = st + c.cost + c.lat
            remaining -= 1
        self.ops = sched
        self.est_time = max(tnow.values())

    def emit(self, nc, block, sems):
        for e in self.ENGS:
            for op in self.ops[e]:
                for d in op.deps:
                    if d.dsem is None:
                        d.inc = True
        for e in self.ENGS:
            c = 0
            for op in self.ops[e]:
                if op.dsem is not None:
                    v = self.dcount[id(op.dsem)] if op.group else op.dval
                    op.ev = (op.dsem, v)
                else:
                    if op.inc:
                        c += 1
                    op.ev = (sems[e], c) if op.inc else None

        def run(e, eng):
            waited = {}
            for op in self.ops[e]:
                for d in op.deps:
                    s, v = d.ev
                    if waited.get(id(s), 0) < v:
                        eng.wait_ge(s, v)
                        waited[id(s)] = v
                if op.fn is None:
                    continue
                ins = op.fn(eng)
                if op.dsem is not None:
                    ins.then_inc(op.dsem, 16)
                elif op.inc:
                    ins.then_inc(sems[e], 1)
            if e == "sp":
                for k, dsem in getattr(self, "dsems", {}).items():
                    eng.wait_ge(dsem, self.dcount[k])

        block.tensor(lambda eng: run("pe", eng))
        block.scalar(lambda eng: run("act", eng))
        block.vector(lambda eng: run("dve", eng))
        block.gpsimd(lambda eng: run("pool", eng))
        block.sync(lambda eng: run("sp", eng))


def bc_last(ap, n):
    l = [list(x) for x in ap.ap]
    assert l[-1][1] == 1
    l[-1] = [0, n]
    return bass.AP(ap.tensor, ap.offset, l)


def bc_new(ap, n):
    l = [list(x) for x in ap.ap] + [[0, n]]
    return bass.AP(ap.tensor, ap.offset, l)


def bc_mid(ap, n):
    l = [list(x) for x in ap.ap]
    l = l[:1] + [[0, n]] + l[1:]
    return bass.AP(ap.tensor, ap.offset, l)


def weight_blocks():
    B = []

    def full(name, w, c0, group):
        B.append(dict(name=name, group=group, F=8 * 512, pieces=[(w, c0, 512, 8, 0, 512)]))

    full("qa", "w_in", C_QA, 0)
    full("ka", "w_in", C_KA, 0)
    full("va0", "w_in", C_VA, 0)
    full("va1", "w_in", C_VA + 512, 0)
    full("ra0", "w_in", C_RA, 0)
    full("ra1", "w_in", C_RA + 512, 0)
    full("qb", "w_in", C_QB, 1)
    full("kb", "w_in", C_KB, 1)
    full("vb", "w_in", C_VB, 1)
    for g in range(4):
        B.append(dict(name=f"mixa{g}", group=2, F=4096,
                      pieces=[("w_in", C_GA + g * 256, 256, 8, 0, 512), ("w_br_a", g * 256, 256, 8, 256, 512)]))
        B.append(dict(name=f"mixb{g}", group=2, F=3072,
                      pieces=[("w_in", C_GB + g * 256, 256, 8, 0, 256), ("w_br_b", g * 256, 256, 4, 2048, 256)]))
    full("wo0", "w_out", 0, 3)
    full("wo1", "w_out", 512, 3)
    for jb in range(NFF // 2):
        pcs = []
        for jj in range(2):
            j = jb * 2 + jj
            pcs.append(("w_up", j * 128, 128, 8, jj * 256, 512))
            pcs.append(("w_up", DFF + j * 128, 128, 8, jj * 256 + 128, 512))
        B.append(dict(name=f"up{jb}", group=4, F=4096, pieces=pcs))
    for half in range(2):
        for kb, (k0, kn) in enumerate(((0, 8), (8, 8), (16, 6))):
            B.append(dict(name=f"dn{half}_{kb}", group=5, F=kn * 512, k0=k0, kn=kn,
                          pieces=[("w_down", half * 512, 512, kn, 0, 512, k0)]))
    full("pg0", "w_ple_gate", 0, 6)
    full("pg1", "w_ple_gate", 512, 6)
    B.append(dict(name="pl", group=6, F=2048, pieces=[("w_ple", 0, 1024, 2, 0, 1024)]))
    off = 0
    for b in B:
        b["off"] = off
        off += 128 * SLOT_F
    return B, off


WSHAPES = dict(w_in=(D, NIN), w_br_a=(D, D), w_br_b=(512, D), w_out=(D, D), w_up=(D, 2 * DFF),
               w_down=(DFF, D), w_ple_gate=(D, D), w_ple=(PLE, D))


def build(nt, debug=False):
    nc = bass.Bass("TRN2", target_bir_lowering=False)
    seq = nt * T
    P = Prog()
    import os as _os
    if _os.environ.get("KLIMIT"):
        P.limit = int(_os.environ["KLIMIT"])
    WB, scr_elems = weight_blocks()
    NBLK = len(WB)

    def din(name, shape, dt=F32):
        return nc.dram_tensor(name, list(shape), dt, kind="ExternalInput")

    def dout(name, shape, dt=F32):
        return nc.dram_tensor(name, list(shape), dt, kind="ExternalOutput")

    xp = din("xp", [seq, D]).ap()
    pp = din("pp", [seq, PLE]).ap()
    xs = din("xs", [TS, D]).ap()
    pps = din("pps", [TS, PLE]).ap()
    ck = din("ck", [512, 512]).ap()
    cv = din("cv", [512, 512]).ap()
    sg = din("sg", [H_A, DK, DV]).ap()
    sc = din("sc", [2, DFF]).ap()
    wd = {k: din(k, v) for k, v in WSHAPES.items()}
    w_a2 = din("w_a2", [16, 512]).ap()
    gvec = din("gvec", [7, D])
    b_a2 = din("b_a2", [512]).ap()
    conv_w = din("conv_w", [3, DFF]).ap()
    conv_b = din("conv_b", [DFF]).ap()
    biasT_d = din("biasT", [5, 2, 128, 512]).ap()
    cmat = din("cmat", [128, 128 + 128 + 512]).ap()

    yp = dout("yp", [seq, D]).ap()
    ys = dout("ys", [TS, D]).ap()
    kp = dout("kp", [512, 512]).ap()
    vp = dout("vp", [512, 512]).ap()
    spo = dout("spo", [H_A, DK, DV]).ap()
    cpo = dout("cpo", [2, DFF]).ap()
    kso = dout("kso", [TS, 512]).ap()
    vso = dout("vso", [TS, 512]).ap()
    sso = dout("sso", [H_A, DK, DV]).ap()
    cso = dout("cso", [2, DFF]).ap()
    scr = nc.dram_tensor("wscr", [scr_elems], BF16, kind="Internal")

    es = contextlib.ExitStack()
    with es:
        def sb(name, shape, dt):
            return es.enter_context(nc.sbuf_tensor("sb_" + name, list(shape), dt))

        xbuf = [sb(f"xbuf{i}", [128, 4, D], F32) for i in range(2)]
        pbuf = sb("pbuf", [128, 4, PLE], F32)
        S_f = sb("S_f", [128, H_A, DV], F32)
        S_b = sb("S_b", [128, H_A, DV], BF16)
        kbT = sb("kbT", [128, 4, 1024], BF16)
        vbr = sb("vbr", [128, 8, 520], BF16)
        biasT = sb("biasT", [128, 5, 2, 512], BF16)
        gpost = sb("gpost", [128, 3, D], F32)
        ggla = sb("ggla", [128, D], F32)
        ident = sb("ident", [128, 128], BF16)
        tri = sb("tri", [128, 128], F32)
        smask = sb("smask", [128, 512], F32)
        gpre = sb("gpre", [128, 3, 8], F32)
        negb = sb("negb", [128, 4], F32)
        cw = sb("cw", [128, 3, NFF], F32)
        cb = sb("cb", [128, NFF], F32)
        walr = sb("walr", [128, 8, 16], BF16)
        wa2 = sb("wa2", [16, 512], BF16)
        ones64 = sb("ones64", [128, 64], BF16)
        carry = sb("carry", [128, NFF, 2], F32)
        wring = [sb(f"wr{i}", [128, SLOT_F], BF16) for i in range(NSLOT)]
        hnT = sb("hnT", [128, 8, T], BF16)
        hntok2 = [sb(f"hntok{i}", [128, D], BF16) for i in range(2)]
        junk2 = [sb(f"junk{i}", [128, D], BF16) for i in range(2)]
        oaT = sb("oaT", [128, 8, T], BF16)
        obT = sb("obT", [128, 4, T], BF16)
        stat2 = sb("stat", [128, 128], F32)
        R1N = 32 * 1024
        R1 = sb("R1", [128, R1N], BF16)
        eps_t = sb("eps_t", [128, 1], F32)
        mhalf = sb("mhalf", [128, 8], F32)
        eblt = sb("eblt", [128, H_A, 4], F32)

        def r1(off_kb, nelem, dt, shape=None):
            esz = 2 if dt == BF16 else 4
            o = off_kb * 512
            n = nelem * esz // 2
            ap = R1[:, o:o + n]
            if dt == F32:
                ap = ap.bitcast(F32)
            ids = [("R1", i) for i in range(off_kb, off_kb + (nelem * esz + 1023) // 1024)]
            return ap, ids

        pd = [es.enter_context(nc.psum_tensor(f"pd{i}", [128, 1024], F32)) for i in range(4)]

        def bank(i):
            return pd[i // 2][:, (i % 2) * 512:(i % 2) * 512 + 512]

        def bankb(i):
            return bank(i).bitcast(BF16)

        def pid(i):
            return ("ps", i)

        sems = {e: es.enter_context(nc.semaphore(f"sem_{e}")) for e in Prog.ENGS}
        s_setup = es.enter_context(nc.semaphore("s_setup"))
        s_setup2 = es.enter_context(nc.semaphore("s_setup2"))
        s_conv = [es.enter_context(nc.semaphore(f"s_conv{i}")) for i in range(NBLK)]
        s_slot = [es.enter_context(nc.semaphore(f"s_slot{i}")) for i in range(NSLOT)]
        s_x = [es.enter_context(nc.semaphore(f"s_x{i}")) for i in range(2)]
        s_p = es.enter_context(nc.semaphore("s_p"))
        s_y = [es.enter_context(nc.semaphore(f"s_y{i}")) for i in range(2)]
        s_ko = es.enter_context(nc.semaphore("s_ko"))
        s_vo = es.enter_context(nc.semaphore("s_vo"))
        s_o = [es.enter_context(nc.semaphore(f"s_o{i}")) for i in range(5)]
        s_l = [es.enter_context(nc.semaphore(f"s_l{i}")) for i in range(3)]
        s_c = [es.enter_context(nc.semaphore(f"s_c{i}")) for i in range(6)]

        CONST = ("const", 0)

        def setup_dma(out, in_, eng="sp", nonc=False):
            def fn(e):
                if nonc:
                    with nc.allow_non_contiguous_dma(reason="small const"):
                        return e.dma_start(out=out, in_=in_)
                return e.dma_start(out=out, in_=in_)
            return P.add(eng, fn, writes=[CONST], dsem=s_setup, group=True)

        cst, cst_ids = r1(40, 768, F32)
        P.add("sp", lambda e: e.dma_start(out=cst, in_=cmat), writes=cst_ids + [CONST], dsem=s_setup, group=True)
        for i, row in enumerate((1, 3, 5)):
            setup_dma(gpost[:, i, :], bass.AP(gvec, row * D, [[0, 128], [1, D]]))
        setup_dma(ggla[:], bass.AP(gvec, 6 * D, [[0, 128], [1, D]]))
        for i, row in enumerate((0, 2, 4)):
            setup_dma(gpre[:, i, :], bass.AP(gvec, row * D, [[1, 128], [128, 8]]), nonc=True)
        setup_dma(negb[:], b_a2.rearrange("(h p) -> p h", p=128), nonc=True)
        setup_dma(cw[:], conv_w.rearrange("i (j p) -> p i j", p=128), nonc=True)
        setup_dma(cb[:], conv_b.rearrange("(j p) -> p j", p=128), nonc=True)
        bst, bst_ids = r1(0, 5 * 2 * 512, F32)
        bst3 = bst.rearrange("p (a b) -> p a b", b=512)
        P.add("sp", lambda e: e.dma_start(out=bst3, in_=biasT_d.rearrange("k r p q -> p (k r) q")),
              writes=bst_ids + [CONST], dsem=s_setup, group=True)
        P.add("pool", lambda e: e.dma_start(out=walr[:], in_=wd["w_in"].ap().rearrange("(kc p) n -> p kc n", p=128)[:, :, C_ALR:C_ALR + 16]),
              writes=[("const", 1)], dsem=s_setup2, group=True)
        P.add("pool", lambda e: e.dma_start(out=wa2[:], in_=w_a2), writes=[("const", 1)], dsem=s_setup2, group=True)

        name2idx = {b["name"]: i for i, b in enumerate(WB)}
        HB = ["va0", "va1"]
        B2B = ["pg0", "pg1", "pl"]
        B1B = [b["name"] for b in WB if b["name"] not in HB + B2B]
        conv_pending = [name2idx[n] for n in HB + B1B + B2B]
        for bi in conv_pending:
            WB[bi]["group"] = bi

        def emit_conv(n=1):
            for _ in range(n):
                if not conv_pending:
                    return
                bi = conv_pending.pop(0)
                b = WB[bi]
                for pc_ in b["pieces"]:
                    wname, c0, ncols, KC, off, kst = pc_[:6]
                    k0 = pc_[6] if len(pc_) > 6 else 0
                    wv = wd[wname].ap().rearrange("(kc p) n -> p kc n", p=128)[:, k0:k0 + KC, c0:c0 + ncols]
                    dst = bass.AP(scr, b["off"] + off, [[SLOT_F, 128], [kst, KC], [1, ncols]])
                    P.add("pool", (lambda e, dst=dst, wv=wv: e.dma_start(out=dst, in_=wv)),
                          writes=[("scr", bi)], dsem=s_conv[bi], group=True, lat=6.0)
        emit_conv(int(_os.environ.get("KCONV0", 4)))

        P.add("dve", lambda e: e.tensor_copy(out=ident[:], in_=cst[:, 0:128]), reads=[CONST] + cst_ids, writes=[("c2", 0)])
        P.add("dve", lambda e: e.tensor_copy(out=tri[:], in_=cst[:, 128:256]), reads=[CONST] + cst_ids, writes=[("c2", 0)])
        P.add("dve", lambda e: e.tensor_copy(out=smask[:], in_=cst[:, 256:768]), reads=[CONST] + cst_ids, writes=[("c2", 0)])
        P.add("dve", lambda e: e.tensor_scalar(out=negb[:], in0=negb[:], scalar1=-1.0, scalar2=None, op0=ALU.mult),
              reads=[CONST], writes=[("c2", 0)])
        P.add("pool", lambda e: e.memset(ones64[:], 1.0), writes=[("c2", 1)])
        P.add("pool", lambda e: e.memset(vbr[:], 1.0), writes=[("vb", i) for i in range(8)])
        P.add("pool", lambda e: e.memset(carry[:], 0.0), writes=[("carryall",)] + [("carry", j) for j in range(NFF)])
        P.add("pool", lambda e: e.memset(S_f[:], 0.0), writes=["S_f"])
        P.add("pool", lambda e: e.memset(S_b[:], 0.0), writes=["S_b"])
        P.add("pool", lambda e: e.memset(eps_t[:], EPS), writes=[("c2", 1)])
        P.add("pool", lambda e: e.memset(mhalf[:], -0.5), writes=[("c2", 1)])
        P.add("dve", lambda e: e.tensor_scalar(out=gpost[:, 0, :], in0=gpost[:, 0, :], scalar1=0.5, scalar2=None, op0=ALU.mult), reads=[CONST], writes=[("c2", 0)])
        P.add("dve", lambda e: e.tensor_scalar(out=gpost[:, 2, :], in0=gpost[:, 2, :], scalar1=0.5, scalar2=None, op0=ALU.mult), reads=[CONST], writes=[("c2", 0)])
        for kb_ in range(5):
            P.add("pool", (lambda e, kb_=kb_: e.tensor_tensor(out=biasT[:, kb_, :, :], in0=bst3[:, 2 * kb_:2 * kb_ + 2, :], in1=bst3[:, 2:4, :], op=ALU.subtract)),
                  reads=bst_ids + [CONST], writes=[("c2", 2)])
        C2 = [CONST, ("const", 1), ("c2", 0), ("c2", 1), ("c2", 2)]

        ring = dict(next_load=0, next_use=0, total=0)
        seqblocks = []

        def ring_issue():
            i = ring["next_load"]
            if i >= len(seqblocks):
                return
            ring["next_load"] += 1
            b = WB[seqblocks[i]]
            slot = i % NSLOT
            F = b["F"]
            src = bass.AP(scr, b["off"], [[SLOT_F, 128], [1, F]])
            P.add("sp", (lambda e, slot=slot, F=F, src=src: e.dma_start(out=wring[slot][:, 0:F], in_=src)),
                  reads=[("scr", b["group"])], writes=[("w", slot)], dsem=s_slot[slot], lat=2.0 + F / 512.0)

        def ring_acquire(name):
            i = ring["next_use"]
            b = WB[seqblocks[i]]
            assert b["name"] == name, (b["name"], name)
            ring["next_use"] += 1
            slot = i % NSLOT
            return wring[slot], ("w", slot)

        def ring_release():
            emit_conv(1)
            ring_issue()

        def small_rstd(ss_ap, n, inv_n, reads, writes, out_ap):
            np_ = ss_ap.shape[0]
            P.add("pool", lambda e: e.tensor_scalar(out=out_ap, in0=ss_ap, scalar1=inv_n, scalar2=EPS, op0=ALU.mult, op1=ALU.add),
                  reads=reads + C2, writes=writes, cost=0.25)
            P.add("pool", lambda e: e.tensor_tensor(out=out_ap, in0=out_ap, in1=mhalf[0:np_, 0:n], op=ALU.pow),
                  reads=writes, writes=writes, cost=0.35)

        rr = dict(i=0)

        def next_bank(choices):
            rr["i"] += 1
            return choices[rr["i"] % len(choices)]

        def emit_tile(tc):
            nb, bt = tc["nb"], tc["bt"]
            TT = nb * bt
            xb = tc["xb"]
            xi = tc["xi"]
            xid = lambda b: ("x", xi, b)
            hid = lambda b: ("hnT", b)
            HN_ALL = [hid(b) for b in range(nb)]
            trb = [6, 7]
            par = tc["par"]
            stat = stat2[:, par * 64:(par + 1) * 64]
            SID = lambda k: ("stat", par, k)
            alt = dict(j=0, h=0)

            def jk():
                alt["j"] += 1
                i = alt["j"] % 2
                return junk2[i], ("junk", i)

            def hk():
                alt["h"] += 1
                i = alt["h"] % 2
                return hntok2[i], ("hntok", i)

            def prenorm(gi, dst_fn=None, dids=None):
                if dst_fn is None:
                    dst_fn = lambda b: hnT[:, :, b * bt:(b + 1) * bt]
                    dids = lambda b: [hid(b)]
                for b in range(nb):
                    jb_, jid = jk()
                    P.add("act", (lambda e, b=b, jb_=jb_: e.activation(out=jb_[0:bt, :], in_=xb[0:bt, b, :], func=AF.Square,
                                                                       accum_out=stat[0:bt, b:b + 1])),
                          reads=[xid(b)], writes=[SID((0, b)), jid], cost=1.25)
                    small_rstd(stat[0:bt, b:b + 1], 1, 1.0 / D, [SID((0, b))], [SID((1, b))], stat[0:bt, 4 + b:5 + b])
                for b in range(nb):
                    hb_, hbid = hk()
                    P.add("act", (lambda e, b=b, hb_=hb_: e.activation(out=hb_[0:bt, :], in_=xb[0:bt, b, :], func=AF.Copy,
                                                                       scale=stat[0:bt, 4 + b:5 + b])),
                          reads=[xid(b), SID((1, b))], writes=[hbid], cost=1.25)
                    bk = next_bank(trb)
                    pv = bankb(bk).rearrange("p (k t) -> p k t", t=128)

                    def trs(e, b=b, pv=pv, hb_=hb_):
                        ins = None
                        for kc in range(8):
                            ins = e.transpose(out=pv[:, kc, 0:bt], in_=hb_[0:bt, kc * 128:(kc + 1) * 128], identity=ident[0:bt, 0:bt])
                        return ins
                    P.add("pe", trs, reads=[hbid] + C2, writes=[pid(bk)], cost=0.9)
                    gb = bc_new(gpre[:, gi, :], bt)
                    P.add("dve", (lambda e, b=b, pv=pv, gb=gb: e.tensor_tensor(out=dst_fn(b), in0=pv[:, :, 0:bt], in1=gb, op=ALU.mult)),
                          reads=[pid(bk)] + C2, writes=dids(b), cost=1.2)

            def proj_F(slot_ap, wid, off, kst, KC, rhs_fn, rhs_ids, col0, banks):
                bk = next_bank(banks)
                ps = bank(bk)

                def fn(e):
                    ins = None
                    for kc in range(KC):
                        ins = e.matmul(ps[:, 0:TT], lhsT=slot_ap[:, off + kc * kst + col0: off + kc * kst + col0 + 128],
                                       rhs=rhs_fn(kc), start=(kc == 0), stop=(kc == KC - 1))
                    return ins
                P.add("pe", fn, reads=[wid] + rhs_ids, writes=[pid(bk)], cost=0.25 * KC)
                return bk, ps

            def proj_T(slot_ap, wid, off, kst, KC, lhs_fn, lhs_ids, b, ncols, banks, col0=0):
                bk = next_bank(banks)
                ps = bank(bk)

                def fn(e):
                    ins = None
                    for kc in range(KC):
                        ins = e.matmul(ps[0:bt, 0:ncols], lhsT=lhs_fn(kc, b),
                                       rhs=slot_ap[:, off + kc * kst + col0: off + kc * kst + col0 + ncols],
                                       start=(kc == 0), stop=(kc == KC - 1))
                    return ins
                P.add("pe", fn, reads=[wid] + lhs_ids, writes=[pid(bk)], cost=0.25 * KC)
                return bk, ps

            hn_rhs = lambda kc: hnT[:, kc, 0:TT]
            hn_lhs = lambda kc, b: hnT[:, kc, b * bt:(b + 1) * bt]
            PB = [0, 1, 2, 3]

            prenorm(0)
            va, va_ids = r1(12, 4 * 1024, BF16)
            va = va.rearrange("p (b c) -> p b c", c=1024)
            alrT, al_ids = r1(28, 512, BF16)
            bk = next_bank(PB)
            ps = bank(bk)

            def alr_fn(e, ps=ps):
                ins = None
                for kc in range(8):
                    ins = e.matmul(ps[0:16, 0:TT], lhsT=walr[:, kc, :], rhs=hnT[:, kc, 0:TT], start=(kc == 0), stop=(kc == 7))
                return ins
            P.add("pe", alr_fn, reads=HN_ALL + C2, writes=[pid(bk)], cost=3.0)
            P.add("act", (lambda e, ps=ps: e.activation(out=alrT[0:16, 0:TT], in_=ps[0:16, 0:TT], func=AF.Copy)),
                  reads=[pid(bk)], writes=al_ids)
            for half in range(2):
                slot, wid = ring_acquire(f"va{half}")
                for b in range(nb):
                    bk, ps = proj_T(slot, wid, 0, 512, 8, hn_lhs, [hid(b)], b, 512, PB)
                    P.add("act", (lambda e, ps=ps, b=b, half=half: e.activation(out=va[0:bt, b, half * 512:(half + 1) * 512], in_=ps[0:bt, :], func=AF.Copy)),
                          reads=[pid(bk)], writes=va_ids)
                ring_release()
            yield "head"
            psrc = tc["p_src"]
            P.add("sp", (lambda e: e.dma_start(out=pbuf[0:bt, 0:nb, :], in_=psrc)), writes=["pbuf"], dsem=s_p, lat=5.0)
            qaT, qa_ids = r1(0, 4 * 512, BF16)
            kaT, ka_ids = r1(4, 4 * 512, BF16)
            katok, kt_ids = r1(8, 4 * 512, BF16)
            silur, sr_ids = r1(20, 8 * 512, BF16)
            eT2 = [r1(29, 512, F32), r1(31, 512, F32), r1(33, 512, F32), r1(35, 512, F32)]
            Bc2 = [r1(37, 512, F32), r1(39, 512, F32)]
            eb2 = [r1(41, 512, F32), r1(43, 512, F32)]
            enb2 = [r1(45, 512, F32), r1(47, 512, F32)]
            ATs2 = [r1(49, 512, BF16), r1(50, 512, BF16)]
            on2 = [r1(51, 1024, BF16), r1(53, 1024, BF16)]
            Stmp, st_ids = r1(55, 1024, F32)
            qaT = qaT.rearrange("p (h t) -> p h t", t=512)
            kaT = kaT.rearrange("p (h t) -> p h t", t=512)
            katok = katok.rearrange("p (b c) -> p b c", c=512)
            silur = silur.rearrange("p (k t) -> p k t", t=512)

            slot_q, wid_q = ring_acquire("qa")
            slot_k, wid_k = None, None
            qbanks = []
            lnq = []
            for h in range(H_A):
                eT, e_ids = eT2[h]
                bk = next_bank(PB)
                ps = bank(bk)
                P.add("pe", (lambda e, ps=ps, h=h: e.matmul(ps[:, 0:TT], lhsT=wa2[0:16, h * 128:(h + 1) * 128], rhs=alrT[0:16, 0:TT],
                                                           start=True, stop=True)),
                      reads=al_ids + C2, writes=[pid(bk)], cost=0.45)
                P.add("act", (lambda e, ps=ps, h=h, eT=eT: e.activation(out=eT[:, 0:TT], in_=ps[:, 0:TT], func=AF.Exp, bias=negb[:, h:h + 1], scale=-1.0)),
                      reads=[pid(bk)] + C2, writes=e_ids + [("phA", par)])
            for h in range(H_A):
                eT, e_ids = eT2[h]
                P.add("act", (lambda e, eT=eT: e.activation(out=eT[:, 0:TT], in_=eT[:, 0:TT], func=AF.Ln, bias=1.0, scale=1.0)),
                      reads=e_ids + [("phA", par)], writes=e_ids + [("phB", par)])
            for h in range(H_A):
                eT, e_ids = eT2[h]
                Bc, B_ids = Bc2[h % 2]
                eb, ebh_ids = eb2[h % 2]
                enb, enb_ids = enb2[h % 2]
                P.add("dve", (lambda e, eT=eT, Bc=Bc: e.tensor_tensor_scan(out=Bc[:, 0:TT], data0=smask[:, 0:TT], data1=eT[:, 0:TT], initial=0.0,
                                                             op0=ALU.mult, op1=ALU.add)),
                      reads=e_ids + C2, writes=B_ids, cost=1.3)
                P.add("act", (lambda e, Bc=Bc, eb=eb: e.activation(out=eb[:, 0:TT], in_=Bc[:, 0:TT], func=AF.Exp, scale=-1.0 / 16.0)),
                      reads=B_ids + [("phB", par)], writes=ebh_ids)
                P.add("pool", (lambda e, h=h, eb=eb: e.tensor_copy(out=eblt[:, h, 0:nb], in_=eb[:, 0:TT].rearrange("p (b t) -> p b t", t=bt)[:, :, bt - 1])),
                      reads=ebh_ids, writes=["eblt"], cost=0.2)
                P.add("act", (lambda e, Bc=Bc, enb=enb: e.activation(out=enb[:, 0:TT], in_=Bc[:, 0:TT], func=AF.Exp, scale=1.0 / 16.0)),
                      reads=B_ids + [("phB", par)], writes=enb_ids)
                bkq, psq = proj_F(slot_q, wid_q, 0, 512, 8, hn_rhs, HN_ALL, h * 128, PB)
                P.add("dve", (lambda e, psq=psq, h=h, eb=eb: e.scalar_tensor_tensor(out=qaT[:, h, 0:TT], in0=psq[:, 0:TT], scalar=float(DK) ** -0.5,
                                                                            in1=eb[:, 0:TT], op0=ALU.mult, op1=ALU.mult)),
                      reads=[pid(bkq)] + ebh_ids, writes=[("R1", h)], cost=0.85)
                if h == H_A - 1:
                    ring_release()
                if h == 0:
                    pass
                if slot_k is None:
                    slot_k, wid_k = ring_acquire("ka")
                bkk, psk = proj_F(slot_k, wid_k, 0, 512, 8, hn_rhs, HN_ALL, h * 128, PB)
                P.add("dve", (lambda e, psk=psk, h=h, enb=enb: e.tensor_tensor(out=kaT[:, h, 0:TT], in0=psk[:, 0:TT], in1=enb[:, 0:TT], op=ALU.mult)),
                      reads=[pid(bkk)] + enb_ids, writes=[("R1", 4 + h)], cost=0.8)
            ring_release()
            for half in range(2):
                slot, wid = ring_acquire(f"ra{half}")
                for c in range(4):
                    bk, ps = proj_F(slot, wid, 0, 512, 8, hn_rhs, HN_ALL, c * 128, PB)
                    ch = half * 4 + c
                    th, th_ids = eT2[c % 2]
                    P.add("act", (lambda e, ps=ps, th=th: e.activation(out=th[:, 0:TT], in_=ps[:, 0:TT], func=AF.Tanh, scale=0.5)),
                          reads=[pid(bk)], writes=th_ids)
                    P.add("dve", (lambda e, ps=ps, ch=ch, th=th: e.scalar_tensor_tensor(out=silur[:, ch, 0:TT], in0=th[:, 0:TT], scalar=1.0, in1=ps[:, 0:TT],
                                                                                    op0=ALU.add, op1=ALU.mult)),
                          reads=[pid(bk)] + th_ids, writes=[("R1", 20 + ch)], cost=0.8)
                ring_release()

            qbT, qb_ids = r1(59, 4 * 512, BF16)
            PT, pt_ids = r1(29, 5 * 2 * 512, BF16)
            kfp, kf_ids = r1(41, 512, F32)
            vfp, vf_ids = r1(43, 512, F32)
            qbT = qbT.rearrange("p (c t) -> p c t", t=512)
            PT = PT.rearrange("p (k r q) -> p k r q", r=2, q=512)
            ch_, ph_ = tc["cur_half"], tc["prev_half"]
            KB_CUR = ("kbT", ch_)
            M2B = [6, 7]
            m2_steps = []
            m2s = {}

            def st_qb(c):
                def f():
                    if c == 0:
                        m2s["qb"] = ring_acquire("qb")
                    slot, wid = m2s["qb"]
                    bk, ps = proj_F(slot, wid, 0, 512, 8, hn_rhs, HN_ALL, c * 128, M2B)
                    P.add("act", (lambda e, ps=ps, c=c: e.activation(out=qbT[:, c, 0:TT], in_=ps[:, 0:TT], func=AF.Copy, scale=float(HD) ** -0.5)),
                          reads=[pid(bk)], writes=qb_ids)
                    if c == 3:
                        ring_release()
                return f

            def st_kb(c):
                def f():
                    if c == 0:
                        m2s["kb"] = ring_acquire("kb")
                    slot, wid = m2s["kb"]
                    bk, ps = proj_F(slot, wid, 0, 512, 8, hn_rhs, HN_ALL, c * 128, M2B)
                    P.add("dve", (lambda e, ps=ps, c=c: e.tensor_copy(out=kbT[:, c, ch_ * 512:ch_ * 512 + TT], in_=ps[:, 0:TT])),
                          reads=[pid(bk)], writes=[KB_CUR])
                    if c == 3:
                        if tc["kv_out"] is not None:
                            ko, vo = tc["kv_out"]
                            for b in range(nb):
                                bk, ps = proj_T(slot, wid, 0, 512, 8, hn_lhs, [hid(b)], b, 512, M2B)
                                P.add("dve", (lambda e, ps=ps: e.tensor_copy(out=kfp[0:bt, :], in_=ps[0:bt, :])), reads=[pid(bk)], writes=kf_ids)
                                P.add("pool", (lambda e, b=b, ko=ko: e.dma_start(out=ko[b * bt:(b + 1) * bt, :], in_=kfp[0:bt, :])),
                                      reads=kf_ids, writes=[("out", "k")], dsem=s_ko)
                        ring_release()
                return f

            def st_vb(b):
                def f():
                    if b == 0:
                        m2s["vb"] = ring_acquire("vb")
                    slot, wid = m2s["vb"]
                    bk, ps = proj_T(slot, wid, 0, 512, 8, hn_lhs, [hid(b)], b, 512, M2B)
                    vs_ = tc["vb_cur"][b]
                    P.add("act", (lambda e, ps=ps, vs_=vs_: e.activation(out=vbr[0:bt, vs_, :].rearrange("p (h e) -> p h e", e=65)[:, :, 0:64],
                                                                         in_=ps[0:bt, :].rearrange("p (h e) -> p h e", e=64), func=AF.Copy)),
                          reads=[pid(bk)], writes=[("vb", vs_)])
                    if tc["kv_out"] is not None:
                        ko, vo = tc["kv_out"]
                        P.add("dve", (lambda e, ps=ps: e.tensor_copy(out=vfp[0:bt, :], in_=ps[0:bt, :])), reads=[pid(bk)], writes=vf_ids)
                        P.add("pool", (lambda e, b=b, vo=vo: e.dma_start(out=vo[b * bt:(b + 1) * bt, :], in_=vfp[0:bt, :])),
                              reads=vf_ids, writes=[("out", "v")], dsem=s_vo)
                    if b == nb - 1:
                        ring_release()
                return f
            m2_steps = [st_qb(c) for c in range(4)] + [st_kb(c) for c in range(4)] + [st_vb(b) for b in range(nb)]

            SB = [int(c) for c in _os.environ.get('KSB', '0123')]

            def attn(qb):
                q0 = qb * bt
                nq = bt
                kbl = []
                for kb in range(5):
                    idx = qb + kb if nb == 4 else kb
                    if idx < 4:
                        if not tc["has_prev"]:
                            continue
                        kbl.append((kb, 128, ph_ * 512 + idx * 128, tc["vb_prev"][idx], ("kbT", ph_)))
                    else:
                        cb_ = idx - 4
                        kbl.append((kb, bt, ch_ * 512 + cb_ * bt, tc["vb_cur"][cb_], KB_CUR))
                for (kb, nk, kc0, vslot, kid) in kbl:
                    for par in range(2):
                        bk = next_bank(SB)
                        ps = bank(bk)
                        psv = ps.rearrange("p (c q) -> p c q", q=128)
                        bT = biasT[0:nk, kb, par, :].rearrange("p (c q) -> p c q", q=128)

                        def sc_fn(e, ps=ps, psv=psv, bT=bT, nk=nk, kc0=kc0, par=par, q0=q0, nq=nq, kb_=kb):
                            ins = None
                            if nq == 128:
                                nob = kb_ in (1, 2)
                                if not nob:
                                    e.matmul(ps[0:nk, :], lhsT=ident[0:nk, 0:nk], rhs=biasT[0:nk, kb_, par, :], start=True, stop=False)
                                for c in range(4):
                                    ins = e.matmul(psv[0:nk, c, 0:nq], lhsT=kbT[par * 64:(par + 1) * 64, c, kc0:kc0 + nk],
                                                   rhs=qbT[par * 64:(par + 1) * 64, c, q0:q0 + nq], start=nob, stop=(nob or c == 3))
                            else:
                                for c in range(4):
                                    e.matmul(psv[0:nk, c, 0:nq], lhsT=ident[0:nk, 0:nk], rhs=bT[:, c, 0:nq], start=True, stop=False)
                                    ins = e.matmul(psv[0:nk, c, 0:nq], lhsT=kbT[par * 64:(par + 1) * 64, c, kc0:kc0 + nk],
                                                   rhs=qbT[par * 64:(par + 1) * 64, c, q0:q0 + nq], start=False, stop=True)
                            return ins
                        P.add("pe", sc_fn, reads=[kid] + qb_ids + C2, writes=[pid(bk)], cost=0.55)
                        PTv = PT[:, kb, par, :].rearrange("p (c q) -> p c q", q=128)
                        P.add("act", (lambda e, psv=psv, PTv=PTv, nk=nk, nq=nq: e.activation(out=PTv[0:nk, :, 0:nq], in_=psv[0:nk, :, 0:nq], func=AF.Exp)),
                              reads=[pid(bk)], writes=[("R1", 29 + kb * 2 + par)])
                pso2 = pd[2]

                def pv_fn(e, kbl=kbl, pso2=pso2, nq=nq):
                    ins = None
                    for h in range(H_B):
                        c, par = h // 2, h % 2
                        o0 = (h // 4) * 512 + (h % 4) * 65
                        for i, (kb, nk, kc0, vslot, kid) in enumerate(kbl):
                            PTv = PT[:, kb, par, :].rearrange("p (c q) -> p c q", q=128)
                            ins = e.matmul(pso2[0:nq, o0:o0 + 65], lhsT=PTv[0:nk, c, 0:nq], rhs=vbr[0:nk, vslot, h * 65:(h + 1) * 65],
                                           start=(i == 0), stop=(i == len(kbl) - 1))
                    return ins
                P.add("pe", pv_fn, reads=pt_ids + [("vb", k[3]) for k in kbl] + C2, writes=[pid(4), pid(5)], cost=0.45 * len(kbl))
                otok, ot_ids = r1(39, 512, BF16)
                for g in range(2):
                    pg = pso2[0:nq, g * 512:g * 512 + 260].rearrange("p (h e) -> p h e", e=65)
                    rv = stat[0:nq, 32 + 4 * g:36 + 4 * g]
                    P.add("dve", (lambda e, pg=pg, rv=rv: e.reciprocal(out=rv, in_=pg[:, :, 64])), reads=[pid(4 + g)], writes=[SID((7, g))], cost=0.2)
                    rvb = bc_new(rv, 64)
                    ov = otok[0:nq, g * 256:(g + 1) * 256].rearrange("p (h e) -> p h e", e=64)
                    P.add("dve", (lambda e, pg=pg, rvb=rvb, ov=ov: e.tensor_tensor(out=ov, in0=pg[:, :, 0:64], in1=rvb, op=ALU.mult)),
                          reads=[pid(4 + g), SID((7, g))], writes=ot_ids, cost=0.45)
                bk = next_bank(trb)
                pv = bankb(bk).rearrange("p (c t) -> p c t", t=128)

                def tra(e, pv=pv, nq=nq):
                    ins = None
                    for c in range(4):
                        ins = e.transpose(out=pv[:, c, 0:nq], in_=otok[0:nq, c * 128:(c + 1) * 128], identity=ident[0:nq, 0:nq])
                    return ins
                P.add("pe", tra, reads=ot_ids + C2, writes=[pid(bk)], cost=0.5)
                P.add("act", (lambda e, pv=pv, nq=nq, q0=q0: e.activation(out=obT[:, :, q0:q0 + nq], in_=pv[:, 0:4, 0:nq], func=AF.Copy)),
                      reads=[pid(bk)], writes=[("obT", qb)], cost=0.6)


            for b in range(nb):
                c0 = b * bt
                ATs, at_ids = ATs2[b % 2]
                on_, on_ids = on2[b % 2]
                pvk = bankb(5).rearrange("p (h c) -> p h c", c=128)

                def trk(e, c0=c0, pvk=pvk):
                    ins = None
                    for h in range(H_A):
                        ins = e.transpose(out=pvk[0:bt, h, :], in_=kaT[:, h, c0:c0 + bt], identity=ident[:, :])
                    return ins
                P.add("pe", trk, reads=ka_ids + C2, writes=[pid(5)], cost=0.5)
                P.add("act", (lambda e, b=b, pvk=pvk: e.activation(out=katok[0:bt, b, :].rearrange("p (h c) -> p h c", c=128), in_=pvk[0:bt, 0:4, :], func=AF.Copy)),
                      reads=[pid(5)], writes=[("R1", 8 + b)], cost=0.6)
                psA = bank(4).rearrange("p (h i) -> p h i", i=128)

                def at_fn(e, c0=c0, psA=psA):
                    ins = None
                    for h in range(H_A):
                        ins = e.matmul(psA[0:bt, h, 0:bt], lhsT=kaT[:, h, c0:c0 + bt], rhs=qaT[:, h, c0:c0 + bt], start=True, stop=True)
                    return ins
                P.add("pe", at_fn, reads=ka_ids + qa_ids, writes=[pid(4)], cost=0.5)
                ATv = ATs.rearrange("p (h i) -> p h i", i=128)
                trib = bc_mid(tri[0:bt, 0:bt], 4)
                P.add("dve", (lambda e, psA=psA, ATv=ATv, trib=trib: e.tensor_tensor(out=ATv[0:bt, :, 0:bt], in0=psA[0:bt, :, 0:bt], in1=trib, op=ALU.mult)),
                      reads=[pid(4)] + C2, writes=at_ids, cost=0.75)
                pso = pd[0]

                def o_fn(e, b=b, c0=c0, ATv=ATv, pso=pso):
                    ins = None
                    for h in range(H_A):
                        e.matmul(pso[0:bt, h * 256:(h + 1) * 256], lhsT=ATv[0:bt, h, 0:bt], rhs=va[0:bt, b, h * 256:(h + 1) * 256], start=True, stop=False)
                        ins = e.matmul(pso[0:bt, h * 256:(h + 1) * 256], lhsT=qaT[:, h, c0:c0 + bt], rhs=S_b[:, h, :], start=False, stop=True)
                    return ins
                VA = lambda b: [("R1", 12 + 2 * b), ("R1", 13 + 2 * b)]
                P.add("pe", o_fn, reads=at_ids + VA(b) + qa_ids + ["S_b"], writes=[pid(0), pid(1)], cost=1.7)
                psS = pd[1]

                def s_fn(e, b=b, psS=psS):
                    ins = None
                    for h in range(H_A):
                        ins = e.matmul(psS[:, h * 256:(h + 1) * 256], lhsT=katok[0:bt, b, h * 128:(h + 1) * 128], rhs=va[0:bt, b, h * 256:(h + 1) * 256],
                                       start=True, stop=True)
                    return ins
                P.add("pe", s_fn, reads=[("R1", 8 + b)] + VA(b), writes=[pid(2), pid(3)], cost=0.85)
                P.add("dve", (lambda e, psS=psS: e.tensor_tensor(out=Stmp, in0=psS[:, :], in1=S_f[:].rearrange("p h v -> p (h v)"), op=ALU.add)),
                      reads=[pid(2), pid(3), "S_f"], writes=st_ids, cost=1.3)
                eblb = bc_last(eblt[:, :, b:b + 1], DV)
                P.add("dve", (lambda e, eblb=eblb: e.tensor_tensor(out=S_f[:], in0=Stmp.rearrange("p (h v) -> p h v", v=DV), in1=eblb, op=ALU.mult)),
                      reads=st_ids + ["eblt"], writes=["S_f"], cost=1.3)

                def scast(e, b=b):
                    ins = None
                    for h in range(H_A):
                        ins = e.activation(out=S_b[:, h, :], in_=Stmp[:, h * DV:(h + 1) * DV], func=AF.Copy, scale=eblt[:, h, b:b + 1])
                    return ins
                P.add("act", scast, reads=st_ids + ["eblt"], writes=["S_b"], cost=1.5)
                for h in range(H_A):
                    jb_, jid = jk()
                    P.add("act", (lambda e, h=h, pso=pso, jb_=jb_: e.activation(out=jb_[0:bt, 0:256], in_=pso[0:bt, h * 256:(h + 1) * 256], func=AF.Square,
                                                                              accum_out=stat[0:bt, 8 + h:9 + h])),
                          reads=[pid(0), pid(1)], writes=[SID(2), jid], cost=0.4)
                small_rstd(stat[0:bt, 8:12], 4, 1.0 / DV, [SID(2)], [SID(3)], stat[0:bt, 12:16])
                for h in range(H_A):
                    P.add("dve", (lambda e, h=h, pso=pso, on_=on_: e.scalar_tensor_tensor(out=on_[0:bt, h * 256:(h + 1) * 256], in0=pso[0:bt, h * 256:(h + 1) * 256],
                                                                                scalar=stat[0:bt, 12 + h:13 + h], in1=ggla[0:bt, h * 256:(h + 1) * 256],
                                                                                op0=ALU.mult, op1=ALU.mult)),
                          reads=[pid(0), pid(1), SID(3)] + C2, writes=on_ids, cost=0.45)
                bk = 4
                pv = bankb(bk).rearrange("p (k t) -> p k t", t=128)

                def tro(e, pv=pv, on_=on_):
                    ins = None
                    for kc in range(8):
                        ins = e.transpose(out=pv[:, kc, 0:bt], in_=on_[0:bt, kc * 128:(kc + 1) * 128], identity=ident[0:bt, 0:bt])
                    return ins
                P.add("pe", tro, reads=on_ids + C2, writes=[pid(bk)], cost=0.9)
                P.add("dve", (lambda e, pv=pv, c0=c0: e.tensor_tensor(out=oaT[:, :, c0:c0 + bt], in0=pv[:, :, 0:bt], in1=silur[:, :, c0:c0 + bt], op=ALU.mult)),
                      reads=[pid(bk)] + sr_ids, writes=[("oaT", b)], cost=1.2)
                for _ in range(min(6, len(m2_steps))):
                    m2_steps.pop(0)()
                if b >= 2 and not _os.environ.get('KATT_END'):
                    attn(b - 2)
            while m2_steps:
                m2_steps.pop(0)()
            for qb_ in range((max(0, nb - 2) if not _os.environ.get('KATT_END') else 0), nb):
                attn(qb_)

            sga2 = [r1(12, 512, F32), r1(14, 512, F32)]
            sgb2 = [r1(16, 512, F32), r1(18, 512, F32)]
            t12 = [r1(20, 512, F32), r1(22, 512, F32)]
            t22 = [r1(24, 512, F32), r1(26, 512, F32)]
            mixT, mx_ids = r1(28, 8 * 512, BF16)
            mixT = mixT.rearrange("p (k t) -> p k t", t=512)
            OA_ALL = [("oaT", b) for b in range(nb)]
            OB_ALL = [("obT", b) for b in range(nb)]
            AB = [0, 1, 2, 3, 4, 5]
            for g in range(4):
                slot_a, wid_a = ring_acquire(f"mixa{g}")
                slot_b, wid_b = ring_acquire(f"mixb{g}")
                for jj in range(2):
                    ch = g * 2 + jj
                    sga, sga_ids = sga2[jj]
                    sgb, sgb_ids = sgb2[jj]
                    t1, t1_ids = t12[jj]
                    t2, t2_ids = t22[jj]
                    bk, ps = proj_F(slot_a, wid_a, 0, 512, 8, hn_rhs, HN_ALL, jj * 128, AB)
                    P.add("act", (lambda e, ps=ps, sga=sga: e.activation(out=sga[:, 0:TT], in_=ps[:, 0:TT], func=AF.Tanh, scale=0.5)), reads=[pid(bk)], writes=sga_ids)
                    bk, ps = proj_F(slot_a, wid_a, 256, 512, 8, (lambda kc: oaT[:, kc, 0:TT]), OA_ALL, jj * 128, AB)
                    P.add("dve", (lambda e, ps=ps, t1=t1, sga=sga: e.scalar_tensor_tensor(out=t1[:, 0:TT], in0=sga[:, 0:TT], scalar=1.0, in1=ps[:, 0:TT], op0=ALU.add, op1=ALU.mult)),
                          reads=[pid(bk)] + sga_ids, writes=t1_ids)
                    bk, ps = proj_F(slot_b, wid_b, 0, 256, 8, hn_rhs, HN_ALL, jj * 128, AB)
                    P.add("act", (lambda e, ps=ps, sgb=sgb: e.activation(out=sgb[:, 0:TT], in_=ps[:, 0:TT], func=AF.Tanh, scale=0.5)), reads=[pid(bk)], writes=sgb_ids)
                    bk, ps = proj_F(slot_b, wid_b, 2048, 256, 4, (lambda kc: obT[:, kc, 0:TT]), OB_ALL, jj * 128, AB)
                    P.add("dve", (lambda e, ps=ps, t2=t2, sgb=sgb: e.scalar_tensor_tensor(out=t2[:, 0:TT], in0=sgb[:, 0:TT], scalar=1.0, in1=ps[:, 0:TT], op0=ALU.add, op1=ALU.mult)),
                          reads=[pid(bk)] + sgb_ids, writes=t2_ids)
                    P.add("dve", (lambda e, ch=ch, t1=t1, t2=t2: e.scalar_tensor_tensor(out=mixT[:, ch, 0:TT], in0=t1[:, 0:TT], scalar=0.5, in1=t2[:, 0:TT], op0=ALU.mult, op1=ALU.add)),
                          reads=t1_ids + t2_ids, writes=[("R1", 28 + ch)], cost=0.8)
                ring_release()
                ring_release()

            msb, m_ids = r1(36, 4 * 1024, F32)
            msb = msb.rearrange("p (b c) -> p b c", c=1024)
            MB = lambda b: [("R1", 36 + 4 * b + i) for i in range(4)]

            def postnorm(gi, from_ps=None, blks=None):
                for b in (range(nb) if blks is None else blks):
                    P.add("dve", (lambda e, b=b: e.tensor_tensor(out=stat[0:bt, 24 + b:25 + b], in0=stat[0:bt, 16 + 2 * b:17 + 2 * b],
                                                                 in1=stat[0:bt, 17 + 2 * b:18 + 2 * b], op=ALU.add)),
                          reads=[SID((4, b))], writes=[SID((5, b))], cost=0.15)
                    small_rstd(stat[0:bt, 24 + b:25 + b], 1, 1.0 / D, [SID((5, b))], [SID((6, b))], stat[0:bt, 28 + b:29 + b])
                    if from_ps is None:
                        P.add("dve", (lambda e, b=b: e.scalar_tensor_tensor(out=msb[0:bt, b, :], in0=msb[0:bt, b, :], scalar=stat[0:bt, 28 + b:29 + b],
                                                                            in1=gpost[0:bt, gi, :], op0=ALU.mult, op1=ALU.mult)),
                              reads=MB(b) + [SID((6, b))] + C2, writes=MB(b), cost=1.3)
                    else:
                        for half in range(2):
                            ps, bk = from_ps[(b, half)]
                            P.add("dve", (lambda e, b=b, half=half, ps=ps: e.scalar_tensor_tensor(
                                out=msb[0:bt, b, half * 512:(half + 1) * 512], in0=ps[0:bt, :], scalar=stat[0:bt, 28 + b:29 + b],
                                in1=gpost[0:bt, gi, half * 512:(half + 1) * 512], op0=ALU.mult, op1=ALU.mult)),
                                reads=[pid(bk), SID((6, b))] + C2, writes=MB(b), cost=0.7)
                    P.add("dve", (lambda e, b=b: e.tensor_tensor(out=xb[0:bt, b, :], in0=xb[0:bt, b, :], in1=msb[0:bt, b, :], op=ALU.add)),
                          reads=MB(b) + [xid(b)], writes=[xid(b)], cost=1.25)

            def evac_T(ps, bk, b, half, sqs=1.0, copy=True):
                jb_, jid = jk()
                P.add("act", (lambda e, ps=ps, b=b, half=half, jb_=jb_: e.activation(out=jb_[0:bt, 0:512], in_=ps[0:bt, :], func=AF.Square, scale=sqs,
                                                                          accum_out=stat[0:bt, 16 + 2 * b + half:17 + 2 * b + half])),
                      reads=[pid(bk)], writes=[SID((4, b)), jid])
                if copy:
                    P.add("dve", (lambda e, ps=ps, b=b, half=half: e.tensor_copy(out=msb[0:bt, b, half * 512:(half + 1) * 512], in_=ps[0:bt, :])),
                          reads=[pid(bk)], writes=MB(b), cost=0.6)

            wo = [ring_acquire("wo0"), ring_acquire("wo1")]
            wo_ps = {}
            for b in range(nb):
                for half in range(2):
                    slot, wid = wo[half]
                    bk, ps = proj_T(slot, wid, 0, 512, 8, (lambda kc, b: mixT[:, kc, b * bt:(b + 1) * bt]), mx_ids, b, 512, [0, 1, 2, 3, 4, 5])
                    evac_T(ps, bk, b, half, 0.5, copy=False)
                    wo_ps[(b, half)] = (ps, bk)
                postnorm(0, wo_ps, [b])
            ring_release()
            ring_release()

            prenorm(1)
            uT, u_ids = r1(0, NFF * 512, BF16)
            uT = uT.rearrange("p (j t) -> p j t", t=512)
            G2 = [r1(22, 640, F32), r1(25, 640, F32)]
            c12 = [r1(28, 512, F32), r1(30, 512, F32)]
            c22 = [r1(32, 512, F32), r1(34, 512, F32)]
            ge2 = [r1(52, 512, F32), r1(54, 512, F32)]
            FB = [0, 1, 2, 3, 4, 5]
            for jb in range(NFF // 2):
                slot, wid = ring_acquire(f"up{jb}")
                for jj in range(2):
                    j = jb * 2 + jj
                    G, g_ids = G2[jj]
                    c1, c1_ids = c12[jj]
                    c2, c2_ids = c22[jj]
                    ge, ge_ids = ge2[jj]
                    FBx = FB if j < NFF - 3 else [2, 3, 4, 5]
                    bka, psa = proj_F(slot, wid, jj * 256, 512, 8, hn_rhs, HN_ALL, 0, FBx)
                    bkg, psg = proj_F(slot, wid, jj * 256 + 128, 512, 8, hn_rhs, HN_ALL, 0, FBx)
                    P.add("pool", (lambda e, j=j, G=G: e.tensor_copy(out=G[:, 0:2], in_=carry[:, j, :])), reads=[("carry", j)], writes=g_ids, cost=0.2)
                    P.add("act", (lambda e, psg=psg, G=G: e.activation(out=G[:, 2:2 + TT], in_=psg[:, 0:TT], func=AF.Copy)), reads=[pid(bkg)], writes=g_ids, cost=0.66)
                    P.add("pool", (lambda e, j=j, G=G: e.tensor_copy(out=carry[:, j, :], in_=G[:, TT:TT + 2])), reads=g_ids, writes=[("carry", j)], cost=0.2)
                    P.add("dve", (lambda e, j=j, G=G, c1=c1: e.tensor_scalar(out=c1[:, 0:TT], in0=G[:, 2:2 + TT], scalar1=cw[:, 2, j:j + 1], scalar2=cb[:, j:j + 1],
                                                                 op0=ALU.mult, op1=ALU.add)),
                          reads=g_ids + C2, writes=c1_ids, cost=0.7)
                    P.add("dve", (lambda e, j=j, G=G, c1=c1, c2=c2: e.scalar_tensor_tensor(out=c2[:, 0:TT], in0=G[:, 1:1 + TT], scalar=cw[:, 1, j:j + 1], in1=c1[:, 0:TT],
                                                                        op0=ALU.mult, op1=ALU.add)),
                          reads=g_ids + c1_ids + C2, writes=c2_ids, cost=0.7)
                    P.add("dve", (lambda e, j=j, G=G, c1=c1, c2=c2: e.scalar_tensor_tensor(out=c1[:, 0:TT], in0=G[:, 0:TT], scalar=cw[:, 0, j:j + 1], in1=c2[:, 0:TT],
                                                                        op0=ALU.mult, op1=ALU.add)),
                          reads=g_ids + c2_ids + C2, writes=c1_ids, cost=0.7)
                    P.add("act", (lambda e, c1=c1, ge=ge: e.activation(out=ge[:, 0:TT], in_=c1[:, 0:TT], func=AF.Gelu_apprx_tanh)), reads=c1_ids, writes=ge_ids, cost=0.62)
                    P.add("dve", (lambda e, j=j, psa=psa, ge=ge: e.tensor_tensor(out=uT[:, j, 0:TT], in0=psa[:, 0:TT], in1=ge[:, 0:TT], op=ALU.mult)),
                          reads=[pid(bka)] + ge_ids, writes=[("R1", j)], cost=0.78)
                ring_release()
            for half in range(2):
                banks = [6, 7, 0, 1] if half == 0 else [2, 3, 4, 5]
                for kbi, (k0, kn) in enumerate(((0, 8), (8, 8), (16, 6))):
                    slot, wid = ring_acquire(f"dn{half}_{kbi}")
                    urd = [("R1", kc) for kc in range(k0, k0 + kn)]
                    if kbi < 2:
                        def dn_fn(e, slot=slot, k0=k0, kn=kn, banks=banks):
                            ins = None
                            for kl in range(kn):
                                kc = k0 + kl
                                for b in range(nb):
                                    ins = e.matmul(bank(banks[b])[0:bt, :], lhsT=uT[:, kc, b * bt:(b + 1) * bt], rhs=slot[:, kl * 512:(kl + 1) * 512],
                                                   start=(kc == 0), stop=(kc == NFF - 1))
                            return ins
                        P.add("pe", dn_fn, reads=[wid] + urd, writes=[pid(banks[b]) for b in range(nb)], cost=0.25 * kn * nb)
                    else:
                        for b in range(nb):
                            def dn_fb(e, slot=slot, k0=k0, kn=kn, banks=banks, b=b):
                                ins = None
                                for kl in range(kn):
                                    kc = k0 + kl
                                    ins = e.matmul(bank(banks[b])[0:bt, :], lhsT=uT[:, kc, b * bt:(b + 1) * bt], rhs=slot[:, kl * 512:(kl + 1) * 512],
                                                   start=(kc == 0), stop=(kc == NFF - 1))
                                return ins
                            P.add("pe", dn_fb, reads=[wid] + urd, writes=[pid(banks[b])], cost=0.25 * kn)
                            evac_T(bank(banks[b]), banks[b], b, half)
                            if half == 1:
                                postnorm(1, blks=[b])
                    ring_release()

            yield "body"
            hnP, hnP_ids = r1(0, 8 * 512, BF16)
            hnP = hnP.rearrange("p (b k t) -> p b k t", k=8, t=128)
            HP = lambda b: [("R1", 2 * b), ("R1", 2 * b + 1)]
            prenorm(2, (lambda b: hnP[:, b, :, 0:bt]), HP)
            pbf, pb_ids = r1(8, 1024, BF16)
            pT, pT_ids = r1(10, 1024, BF16)
            sgp2 = [r1(52, 512, F32), r1(54, 512, F32)]
            pT = pT.rearrange("p (k t) -> p k t", t=512)
            pbf = pbf.rearrange("p (b c) -> p b c", c=256)
            P.add("pool", (lambda e: e.tensor_copy(out=pbf[0:bt, 0:nb, :], in_=pbuf[0:bt, 0:nb, :])), reads=["pbuf"], writes=pb_ids, cost=1.9)
            for b in range(nb):
                bk = next_bank(trb)
                pv = bankb(bk).rearrange("p (k t) -> p k t", t=128)

                def trp(e, b=b, pv=pv):
                    ins = None
                    for kc in range(2):
                        ins = e.transpose(out=pv[:, kc, 0:bt], in_=pbf[0:bt, b, kc * 128:(kc + 1) * 128], identity=ident[0:bt, 0:bt])
                    return ins
                P.add("pe", trp, reads=pb_ids + C2, writes=[pid(bk)], cost=0.25)
                P.add("dve", (lambda e, b=b, pv=pv: e.tensor_copy(out=pT[:, :, b * bt:(b + 1) * bt], in_=pv[:, 0:2, 0:bt])), reads=[pid(bk)], writes=pT_ids, cost=0.4)
            slots_g = [ring_acquire("pg0"), ring_acquire("pg1")]
            slot_l, wid_l = ring_acquire("pl")
            k_ = 0
            for b in range(nb):
                for half in range(2):
                    slot, wid = slots_g[half]
                    sgp, sgp_ids = sgp2[k_ % 2]
                    k_ += 1
                    bkg, psg = proj_T(slot, wid, 0, 512, 8, (lambda kc, b: hnP[:, b, kc, 0:bt]), HP(b), b, 512, [0, 1, 2, 3, 4, 5])
                    bkv, psv = proj_T(slot_l, wid_l, 0, 1024, 2, (lambda kc, b: pT[:, kc, b * bt:(b + 1) * bt]), pT_ids, b, 512, [0, 1, 2, 3, 4, 5], col0=half * 512)
                    P.add("act", (lambda e, psg=psg, sgp=sgp: e.activation(out=sgp[0:bt, :], in_=psg[0:bt, :], func=AF.Tanh, scale=0.5)), reads=[pid(bkg)], writes=sgp_ids)
                    P.add("dve", (lambda e, psv=psv, b=b, half=half, sgp=sgp: e.scalar_tensor_tensor(out=msb[0:bt, b, half * 512:(half + 1) * 512], in0=sgp[0:bt, :], scalar=1.0, in1=psv[0:bt, :], op0=ALU.add, op1=ALU.mult)),
                          reads=[pid(bkv)] + sgp_ids, writes=MB(b), cost=0.78)
                    jb_, jid = jk()
                    P.add("act", (lambda e, b=b, half=half, jb_=jb_: e.activation(out=jb_[0:bt, 0:512], in_=msb[0:bt, b, half * 512:(half + 1) * 512], func=AF.Square, scale=0.5,
                                                                       accum_out=stat[0:bt, 16 + 2 * b + half:17 + 2 * b + half])),
                          reads=MB(b), writes=[SID((4, b)), jid])
                postnorm(2, blks=[b])
            ring_release()
            ring_release()
            ring_release()
            ydst = tc["y_dst"]
            P.add("pool", (lambda e: e.dma_start(out=ydst, in_=xb[0:bt, 0:nb, :])), reads=[xid(b) for b in range(nb)],
                  writes=[("out", "y", xi)], dsem=s_y[xi])

        name2idx = {b["name"]: i for i, b in enumerate(WB)}
        HB = ["va0", "va1"]
        B2B = ["pg0", "pg1", "pl"]
        B1B = [b["name"] for b in WB if b["name"] not in HB + B2B]
        order = list(HB)
        for ti in range(nt):
            order += B1B
            if ti + 1 < nt:
                order += HB
            order += B2B
        order += HB + B1B + B2B
        seqblocks.extend(name2idx[n] for n in order)
        for _ in range(NSLOT):
            ring_issue()

        def load_x(ti):
            xi = ti % 2
            src = xp[ti * T:(ti + 1) * T, :].rearrange("(b p) d -> p b d", p=128)
            P.add("sp", (lambda e: e.dma_start(out=xbuf[xi][:], in_=src)), reads=[("out", "y", xi)],
                  writes=[("x", xi, b) for b in range(4)], dsem=s_x[xi])

        def load_xs():
            xi = nt % 2
            P.add("sp", (lambda e, xi=xi: e.dma_start(out=xbuf[xi][0:TS, 0, :], in_=xs)), reads=[("out", "y", xi)],
                  writes=[("x", xi, 0)], dsem=s_x[xi])

        def mk_tc(ti):
            last = (ti == nt - 1)
            return dict(nb=4, bt=128, par=ti % 2, xb=xbuf[ti % 2], xi=ti % 2, cur_half=ti % 2, prev_half=1 - ti % 2,
                        vb_cur=[(4 * ti + b) % 8 for b in range(4)], vb_prev=[(4 * (ti - 1) + b) % 8 for b in range(4)],
                        has_prev=(ti > 0), kv_out=((kp, vp) if last else None),
                        p_src=pp[ti * T:(ti + 1) * T, :].rearrange("(b p) d -> p b d", p=128),
                        y_dst=yp[ti * T:(ti + 1) * T, :].rearrange("(b p) d -> p b d", p=128))

        load_x(0)
        if nt > 1:
            load_x(1)
        else:
            load_xs()
        gens = {0: emit_tile(mk_tc(0))}
        next(gens[0])
        for ti in range(nt):
            next(gens[ti])
            if ti + 1 < nt:
                gens[ti + 1] = emit_tile(mk_tc(ti + 1))
                next(gens[ti + 1])
            for _ in gens[ti]:
                pass
            if ti + 2 < nt:
                load_x(ti + 2)
            elif ti + 2 == nt:
                load_xs()

        P.add("pool", (lambda e: e.dma_start(out=spo.rearrange("h c v -> c h v"), in_=S_f[:])), reads=["S_f"], writes=[("out", "sp")], dsem=s_o[0])

        def conv_store(dst, key, sems2):
            for r in range(2):
                def fn(e, r=r):
                    with nc.allow_non_contiguous_dma(reason="tiny conv state"):
                        return e.dma_start(out=dst[r, :].rearrange("(j p) -> p j", p=128), in_=carry[:, :, r])
                P.add("pool", fn, reads=[("carry", j) for j in range(NFF)], writes=[("out", key, r), ("carryall",)], dsem=sems2[r])
        conv_store(cpo, "cp", s_c[0:2])
        P.add("sp", (lambda e: e.dma_start(out=S_f[:], in_=sg.rearrange("h c v -> c h v"))), reads=[("out", "sp")], writes=["S_f"], dsem=s_l[0])
        P.add("act", (lambda e: e.activation(out=S_b[:], in_=S_f[:], func=AF.Copy)), reads=["S_f"], writes=["S_b"])

        for r in range(2):
            def cld(e, r=r):
                with nc.allow_non_contiguous_dma(reason="tiny conv state"):
                    return e.dma_start(out=carry[:, :, r], in_=sc[r, :].rearrange("(j p) -> p j", p=128))
            P.add("pool", cld, reads=[("carryall",)], writes=[("carry", j) for j in range(NFF)] + [("carryall",)], dsem=s_c[2 + r])
        ckf, ckf_ids = r1(36, 4 * 512, F32)
        cvf, cvf_ids = r1(44, 4 * 512, F32)
        ckb, ckb_ids = r1(52, 4 * 512 // 2, BF16)
        ckf = ckf.rearrange("p (b c) -> p b c", c=512)
        cvf = cvf.rearrange("p (b c) -> p b c", c=512)
        P.add("sp", (lambda e: e.dma_start(out=ckf, in_=ck.rearrange("(b p) c -> p b c", p=128))), writes=ckf_ids, dsem=s_l[1])
        P.add("sp", (lambda e: e.dma_start(out=cvf, in_=cv.rearrange("(b p) c -> p b c", p=128))), writes=cvf_ids, dsem=s_l[2])
        ckb2 = ckb.rearrange("p (b c) -> p b c", c=512)
        for b2 in range(2):
            P.add("dve", (lambda e, b2=b2: e.tensor_copy(out=ckb2[:, :, :], in_=ckf[:, 2 * b2:2 * b2 + 2, :])), reads=ckf_ids, writes=ckb_ids)
            for bb in range(2):
                b = 2 * b2 + bb
                bk = 6 + bb
                pv = bankb(bk).rearrange("p (c t) -> p c t", t=128)

                def trc(e, bb=bb, pv=pv):
                    ins = None
                    for c in range(4):
                        ins = e.transpose(out=pv[:, c, :], in_=ckb2[:, bb, c * 128:(c + 1) * 128], identity=ident[:, :])
                    return ins
                P.add("pe", trc, reads=ckb_ids + C2, writes=[pid(bk)])
                P.add("dve", (lambda e, b=b, pv=pv: e.tensor_copy(out=kbT[:, :, 512 + b * 128:512 + (b + 1) * 128], in_=pv[:, 0:4, :])),
                      reads=[pid(bk)], writes=[("kbT", 1)])
        for b in range(4):
            P.add("pool", (lambda e, b=b: e.tensor_copy(out=vbr[:, 4 + b, :].rearrange("p (h e) -> p h e", e=65)[:, :, 0:64],
                                                        in_=cvf[:, b, :].rearrange("p (h e) -> p h e", e=64))), reads=cvf_ids, writes=[("vb", 4 + b)])

        tcs = dict(nb=1, bt=TS, par=nt % 2, xb=xbuf[nt % 2], xi=nt % 2, cur_half=0, prev_half=1,
                   vb_cur=[0], vb_prev=[4, 5, 6, 7], has_prev=True, kv_out=(kso, vso),
                   p_src=pps.rearrange("(b p) d -> p b d", p=TS), y_dst=ys.rearrange("(b p) d -> p b d", p=TS))
        for _ in emit_tile(tcs):
            pass
        P.add("pool", (lambda e: e.dma_start(out=sso.rearrange("h c v -> c h v"), in_=S_f[:])), reads=["S_f"], writes=[("out", "ss")], dsem=s_o[3])
        conv_store(cso, "cs", s_c[4:6])
        P.add("sp", None, reads=[("out", "y", 0), ("out", "y", 1), ("out", "k"), ("out", "v"), ("out", "sp"), ("out", "cp", 0), ("out", "cp", 1),
                                 ("out", "ss"), ("out", "cs", 0), ("out", "cs", 1)])
        print("total ops", P.nadd)
        P.dbg = bool(_os.environ.get("KDBG"))
        if not _os.environ.get("KNOSCHED"):
            W_ = int(_os.environ.get("KW_PE", 24))
            P.schedule(dict(pe=W_, act=W_ * 2 // 3, dve=W_ * 2 // 3, pool=W_ * 2 // 3, sp=8))
            print("sched est time us", P.est_time)
        block = es.enter_context(nc.Block())
        P.emit(nc, block, sems)
    return nc


def _bias_tiles(rel_bias):
    kb = np.arange(5)[:, None, None]
    key = np.arange(128)[None, :, None]
    q = np.arange(128)[None, None, :]
    d = 512 + q - kb * 128 - key
    idx = np.clip(d, -128, 128) + 128
    kc_rel = (kb * 128 + key) // 64
    qc_rel = 8 + q // 64
    vis = (kc_rel <= qc_rel) & (kc_rel >= qc_rel - 8)
    vis = np.broadcast_to(vis, idx.shape)
    out = np.empty((5, 2, 128, 4, 128), np.float32)
    for par in range(2):
        for c in range(4):
            h = 2 * c + par
            vals = rel_bias[h][idx]
            out[:, par, :, c, :] = np.where(vis, vals, np.float32(NEG))
    return out.reshape(5, 2, 128, 512)


_NC_CACHE = {}


def kernel(x_prompt, x_sample, cache_attn_k, cache_attn_v, state_gla, state_conv,
           p_prompt, p_sample, g_pre_mix, w_in, w_a2, b_a2, g_gla, rel_bias,
           w_br_a, w_br_b, w_out, g_post_mix, g_pre_ffn, w_up, conv_w, conv_b,
           w_down, g_post_ffn, g_pre_ple, w_ple_gate, w_ple, g_post_ple):
    f = lambda a: np.ascontiguousarray(np.asarray(a, dtype=np.float32))
    x_prompt = f(x_prompt)
    seq = x_prompt.shape[1]
    nt = seq // T
    ncores = x_prompt.shape[0]
    if nt not in _NC_CACHE:
        _NC_CACHE[nt] = build(nt)
    nc = _NC_CACHE[nt]
    gvec = np.stack([f(g_pre_mix)[0], f(g_post_mix)[0], f(g_pre_ffn)[0], f(g_post_ffn)[0],
                     f(g_pre_ple)[0], f(g_post_ple)[0], f(g_gla)[0]], axis=0)
    cmat = np.zeros((128, 768), np.float32)
    cmat[:, 0:128] = np.eye(128, dtype=np.float32)
    cmat[:, 128:256] = np.triu(np.ones((128, 128), np.float32))
    sm = np.ones((512,), np.float32)
    sm[::128] = 0.0
    cmat[:, 256:768] = sm[None, :]
    shared = dict(w_in=f(w_in)[0], w_a2=f(w_a2)[0], w_br_a=f(w_br_a)[0], w_br_b=f(w_br_b)[0], w_out=f(w_out)[0],
                  w_up=f(w_up)[0], w_down=f(w_down)[0], w_ple_gate=f(w_ple_gate)[0], w_ple=f(w_ple)[0],
                  gvec=gvec, b_a2=f(b_a2)[0], conv_w=f(conv_w)[0], conv_b=f(conv_b)[0],
                  biasT=_bias_tiles(f(rel_bias)[0]), cmat=cmat)
    in_maps = []
    for c in range(ncores):
        m = dict(shared)
        m.update(xp=x_prompt[c], pp=f(p_prompt)[0, c], xs=f(x_sample)[c], pps=f(p_sample)[0, c],
                 ck=f(cache_attn_k)[0, c].reshape(512, 512), cv=f(cache_attn_v)[0, c].reshape(512, 512),
                 sg=f(state_gla)[0, c], sc=f(state_conv)[0, c])
        in_maps.append(m)
    res = run_bass_kernel_spmd(nc, in_maps, core_ids=list(range(ncores)))
    R = res.results
    st = lambda k: np.stack([np.asarray(r[k], dtype=np.float32) for r in R], axis=0)
    keep = min(512, seq)
    yp = st("yp")
    ys = st("ys")
    kpo = st("kp").reshape(ncores, keep, H_B, HD)[None]
    vpo = st("vp").reshape(ncores, keep, H_B, HD)[None]
    spo = st("spo")[None]
    cpo = st("cpo")[None]
    kso = st("kso").reshape(ncores, TS, H_B, HD)[None]
    vso = st("vso").reshape(ncores, TS, H_B, HD)[None]
    sso = st("sso")[None]
    cso = st("cso")[None]
    return (yp, ys, kpo, vpo, spo, cpo, kso, vso, sso, cso)
```
